# Optimizing a Trainium2 kernel written in Bass

```python
import math
import jax, jax.numpy as jnp
from jax import lax
import numpy as np

D_MODEL = 2048
BATCH = 4
SEQ = 4096
DEPTH = 4

GRID_W = 64
CTX_LEN = 256
EPS = 1e-6
CONV_K = 3
CHUNK = 128
N_BRANCH = 4
BRANCH_WIDTH = D_MODEL // 2
SSD_INNER = BRANCH_WIDTH
SSD_HEAD_DIM = 64
SSD_HEADS = SSD_INNER // SSD_HEAD_DIM
SSD_GROUPS = 4
SSD_STATE = 128
SSD_CONV_CH = SSD_INNER + 2 * SSD_GROUPS * SSD_STATE
POOL_WIDTH = BRANCH_WIDTH
POOL_WINDOWS = (2, 4, 8, 16)
N_POOL = len(POOL_WINDOWS)
POOL_GROUP = POOL_WIDTH // N_POOL
SC_WIDTH = BRANCH_WIDTH
RET_V = BRANCH_WIDTH
RET_HEADS = 8
RET_V_HEAD = RET_V // RET_HEADS
RET_K_HEAD = RET_V_HEAD // 2
RET_QK = RET_HEADS * RET_K_HEAD
ROPE_BASE = 10000.0
D_FF = 256 * ((8 * D_MODEL // 3 + 255) // 256)
SPLIT_SIZES = (SSD_INNER, SSD_CONV_CH, SSD_HEADS, POOL_WIDTH, SC_WIDTH, SC_WIDTH, SC_WIDTH,
               RET_QK, RET_QK, RET_V, RET_V)
IN_COLS = sum(SPLIT_SIZES)
SPLIT_OFFSETS = tuple(int(o) for o in np.cumsum(SPLIT_SIZES)[:-1])

kernel_name = 'hybrid_ssd_pool_shortconv_retention_dit'


def rmsnorm(x, w):
    xf = x.astype(jnp.float32)
    y = xf * lax.rsqrt(jnp.mean(jnp.square(xf), axis=-1, keepdims=True) + EPS)
    return (y * w).astype(x.dtype)


def head_layernorm(y):
    yf = y.astype(jnp.float32)
    mu = jnp.mean(yf, axis=-1, keepdims=True)
    var = jnp.mean(jnp.square(yf - mu), axis=-1, keepdims=True)
    return ((yf - mu) * lax.rsqrt(var + EPS)).astype(y.dtype)


def modulate(h, shift, scale):
    return h * (1.0 + scale) + shift


def dwconv_centered(u, w):
    k_taps = w.shape[0]
    pad = k_taps // 2
    n = u.shape[1]
    up = jnp.pad(u, ((0, 0), (pad, pad), (0, 0)))
    y = up[:, 0:n] * w[0]
    for j in range(1, k_taps):
        y = y + up[:, j:j + n] * w[j]
    return y


def chunked_scan(q, k, v, log_a, h0):
    bsz, n, nh, dn = q.shape
    dp = v.shape[-1]
    nc = n // CHUNK
    f32 = jnp.float32
    qc = q.astype(f32).reshape(bsz, nc, CHUNK, nh, dn)
    kc = k.astype(f32).reshape(bsz, nc, CHUNK, nh, dn)
    vc = v.astype(f32).reshape(bsz, nc, CHUNK, nh, dp)
    cum = jnp.cumsum(log_a.astype(f32).reshape(bsz, nc, CHUNK, nh), axis=2)
    causal = jnp.tril(jnp.ones((CHUNK, CHUNK), dtype=bool))[None, None, :, :, None]
    seg = cum[:, :, :, None, :] - cum[:, :, None, :, :]
    decay = jnp.exp(jnp.where(causal, seg, -jnp.inf))
    scores = jnp.einsum('bcihn,bcjhn->bcijh', qc, kc) * decay
    y_intra = jnp.einsum('bcijh,bcjhp->bcihp', scores, vc)
    last = cum[:, :, -1:, :]
    w_in = jnp.exp(last - cum)
    states = jnp.einsum('bcjhn,bcjhp->bchnp', kc * w_in[..., None], vc)
    chunk_decay = jnp.exp(last[:, :, 0, :])

    def step(h, inp):
        s, d = inp
        return h * d[:, :, None, None] + s, h

    h_final, h_enter = lax.scan(step, h0.astype(f32),
                                (jnp.moveaxis(states, 1, 0), jnp.moveaxis(chunk_decay, 1, 0)))
    h_enter = jnp.moveaxis(h_enter, 0, 1)
    y_inter = jnp.einsum('bcihn,bchnp->bcihp', qc, h_enter) * jnp.exp(cum)[..., None]
    y = (y_intra + y_inter).reshape(bsz, n, nh, dp)
    return y.astype(q.dtype), h_final


def bidir_scan(q, k, v_f, v_b, la_f, la_b, h0_f, h0_b):
    y_f, h_f = chunked_scan(q, k, v_f, la_f, h0_f)
    fl = lambda a: jnp.flip(a, axis=1)
    y_b, h_b = chunked_scan(fl(q), fl(k), fl(v_b), fl(la_b), h0_b)
    return y_f + fl(y_b), h_f, h_b


def axial_rope(t, rows):
    dh = t.shape[-1]
    half = dh // 2
    quarter = half // 2
    f32 = jnp.float32
    row = jnp.repeat(jnp.arange(rows), GRID_W).astype(f32)
    col = jnp.tile(jnp.arange(GRID_W), rows).astype(f32)
    inv = ROPE_BASE ** (-jnp.arange(quarter, dtype=f32) / quarter)

    def rot(xp, pos):
        ang = pos[:, None] * inv[None, :]
        cos = jnp.cos(ang)[None, :, None, :]
        sin = jnp.sin(ang)[None, :, None, :]
        x1 = xp[..., :quarter].astype(f32)
        x2 = xp[..., quarter:].astype(f32)
        return jnp.concatenate([x1 * cos - x2 * sin, x1 * sin + x2 * cos], axis=-1)

    return jnp.concatenate([rot(t[..., :half], row), rot(t[..., half:], col)], axis=-1).astype(t.dtype)


def multiscale_pool(u, pool_w, pool_scale):
    bsz, n, _ = u.shape
    f32 = jnp.float32
    g = u.reshape(bsz, n, N_POOL, POOL_GROUP).astype(f32)
    cs = jnp.pad(jnp.cumsum(g, axis=1), ((0, 0), (1, 0), (0, 0), (0, 0)))
    t = jnp.arange(n)
    means = []
    for gi, win in enumerate(POOL_WINDOWS):
        lo = jnp.clip(t - win // 2, 0, n)
        hi = jnp.clip(t - win // 2 + win, 0, n)
        csg = cs[:, :, gi]
        s = csg[:, hi] - csg[:, lo]
        means.append(s / (hi - lo).astype(f32)[None, :, None])
    pooled = (jnp.stack(means, axis=2) - g).astype(u.dtype)
    y = jnp.einsum('blgc,gcd->blgd', pooled, pool_w)
    return y.reshape(bsz, n, POOL_WIDTH) * pool_scale


def stream_features(h, lp, rows):
    bsz, n, _ = h.shape
    f32 = jnp.float32
    u = h @ lp['w_in']
    z, xbc, dt_raw, pool_in, sc_b, sc_c, sc_x, rq, rk, rv, rg = jnp.split(u, SPLIT_OFFSETS, axis=-1)
    xbc = jax.nn.silu(dwconv_centered(xbc, lp['ssd_conv_w']) + lp['ssd_conv_b'])
    xs, bm, cm = jnp.split(xbc, (SSD_INNER, SSD_INNER + SSD_GROUPS * SSD_STATE), axis=-1)
    xh = xs.reshape(bsz, n, SSD_HEADS, SSD_HEAD_DIM)
    rep = SSD_HEADS // SSD_GROUPS
    bm = jnp.repeat(bm.reshape(bsz, n, SSD_GROUPS, SSD_STATE), rep, axis=2)
    cm = jnp.repeat(cm.reshape(bsz, n, SSD_GROUPS, SSD_STATE), rep, axis=2)
    dt = jax.nn.softplus(dt_raw.astype(f32)[:, :, None, :] + lp['ssd_dt_bias'].astype(f32))
    a = -jnp.exp(lp['ssd_a_log'].astype(f32))
    la = dt * a
    v = xh[:, :, None] * dt[..., None]
    ssd = (cm, bm, v[:, :, 0], v[:, :, 1], la[:, :, 0], la[:, :, 1])
    q = rq.reshape(bsz, n, RET_HEADS, RET_K_HEAD) * (RET_K_HEAD ** -0.5)
    k = rk.reshape(bsz, n, RET_HEADS, RET_K_HEAD)
    if rows is not None:
        q = axial_rope(q, rows)
        k = axial_rope(k, rows)
    vr = rv.reshape(bsz, n, RET_HEADS, RET_V_HEAD)
    la_r = jnp.broadcast_to(jax.nn.log_sigmoid(lp['ret_decay_logit'].astype(f32)), (bsz, n, 2, RET_HEADS))
    ret = (q, k, vr, vr, la_r[:, :, 0], la_r[:, :, 1])
    return dict(ssd=ssd, ret=ret, xh=xh, z=z, pool_in=pool_in, sc=(sc_b, sc_c, sc_x), rg=rg)


def mixer_out(h, f, y_ssd_scan, y_ret_scan, lp):
    bsz, n, _ = h.shape
    y = (y_ssd_scan + f['xh'] * lp['ssd_d'][:, None]).reshape(bsz, n, SSD_INNER)
    y_ssd = rmsnorm(y * jax.nn.silu(f['z']), lp['ssd_norm_w'])
    y_pool = multiscale_pool(f['pool_in'], lp['pool_w'], lp['pool_scale'])
    sc_b, sc_c, sc_x = f['sc']
    y_sc = sc_b * dwconv_centered(sc_c * sc_x, lp['sconv_w'])
    y_ret = jax.nn.silu(f['rg']) * head_layernorm(y_ret_scan).reshape(bsz, n, RET_V)
    merged = None
    for i, yb in enumerate((y_ssd, y_pool, y_sc, y_ret)):
        gate = jax.nn.sigmoid(h @ lp['w_gate'][i] + lp['b_gate'][i])
        term = gate * (yb @ lp['w_branch'][i])
        merged = term if merged is None else merged + term
    return merged @ lp['w_o']


def conv_ffn(h, lp):
    u = dwconv_centered(h @ lp['ffn_up'], lp['ffn_conv_w']) + lp['ffn_conv_b']
    a, b = jnp.split(u, 2, axis=-1)
    return (jax.nn.silu(a) * b) @ lp['ffn_down']


def setup_inputs(seed: int = 0) -> dict:
    key = jax.random.key(seed)
    k = jax.random.split(key, 28)
    f32 = jnp.float32
    L, D = DEPTH, D_MODEL

    def nrm(kk, shape, scale):
        return jax.random.normal(kk, shape, f32) * scale

    def gain(kk, shape):
        return 1.0 + nrm(kk, shape, 0.02)

    dt0 = jnp.exp(jax.random.uniform(k[11], (L, 2, SSD_HEADS), f32, math.log(1e-3), math.log(1e-1)))
    gam = 1.0 - jnp.exp2(-(5.0 + jnp.arange(RET_HEADS, dtype=f32)))
    gam_logit = jnp.log(gam) - jnp.log1p(-gam)
    return {
        'x': nrm(k[0], (BATCH, SEQ, D), 1.0),
        'c': nrm(k[1], (BATCH, D), 1.0),
        'ctx': nrm(k[2], (BATCH, CTX_LEN, D), 1.0),
        'c_ctx': nrm(k[3], (D,), 1.0),
        'w_mod': nrm(k[4], (L, D, 6 * D), 0.5 * D ** -0.5),
        'b_mod': nrm(k[5], (L, 6 * D), 0.01),
        'norm1_w': gain(k[6], (L, D)),
        'w_in': nrm(k[7], (L, D, IN_COLS), D ** -0.5),
        'ssd_conv_w': nrm(k[8], (L, CONV_K, SSD_CONV_CH), CONV_K ** -0.5),
        'ssd_conv_b': nrm(k[9], (L, SSD_CONV_CH), 0.01),
        'ssd_a_log': jnp.log(jax.random.uniform(k[10], (L, 2, SSD_HEADS), f32, 1.0, 16.0)),
        'ssd_dt_bias': dt0 + jnp.log(-jnp.expm1(-dt0)),
        'ssd_d': 1.0 + nrm(k[12], (L, SSD_HEADS), 0.1),
        'ssd_norm_w': gain(k[13], (L, SSD_INNER)),
        'pool_w': nrm(k[14], (L, N_POOL, POOL_GROUP, POOL_GROUP), POOL_GROUP ** -0.5),
        'pool_scale': gain(k[15], (L, POOL_WIDTH)),
        'sconv_w': nrm(k[16], (L, CONV_K, SC_WIDTH), CONV_K ** -0.5),
        'ret_decay_logit': jnp.broadcast_to(gam_logit, (L, 2, RET_HEADS)) + nrm(k[17], (L, 2, RET_HEADS), 0.05),
        'w_branch': nrm(k[18], (L, N_BRANCH, BRANCH_WIDTH, D), BRANCH_WIDTH ** -0.5),
        'w_gate': nrm(k[19], (L, N_BRANCH, D, D), D ** -0.5),
        'b_gate': nrm(k[20], (L, N_BRANCH, D), 0.01),
        'w_o': nrm(k[21], (L, D, D), D ** -0.5),
        'norm2_w': gain(k[22], (L, D)),
        'ffn_up': nrm(k[23], (L, D, 2 * D_FF), D ** -0.5),
        'ffn_conv_w': nrm(k[24], (L, CONV_K, 2 * D_FF), CONV_K ** -0.5),
        'ffn_conv_b': nrm(k[25], (L, 2 * D_FF), 0.01),
        'ffn_down': nrm(k[26], (L, D_FF, D), D_FF ** -0.5),
        'final_norm_w': gain(k[27], (D,)),
    }


def reference(x, c, ctx, c_ctx, w_mod, b_mod, norm1_w, w_in, ssd_conv_w, ssd_conv_b, ssd_a_log,
              ssd_dt_bias, ssd_d, ssd_norm_w, pool_w, pool_scale, sconv_w, ret_decay_logit,
              w_branch, w_gate, b_gate, w_o, norm2_w, ffn_up, ffn_conv_w, ffn_conv_b, ffn_down,
              final_norm_w):
    bsz, n_lat, _ = x.shape
    rows = n_lat // GRID_W
    xl, xc = x, ctx
    ssd0 = jnp.zeros((bsz, SSD_HEADS, SSD_STATE, SSD_HEAD_DIM), jnp.float32)
    ret0 = jnp.zeros((bsz, RET_HEADS, RET_K_HEAD, RET_V_HEAD), jnp.float32)
    for l in range(DEPTH):
        lp = dict(w_in=w_in[l], ssd_conv_w=ssd_conv_w[l], ssd_conv_b=ssd_conv_b[l],
                  ssd_a_log=ssd_a_log[l], ssd_dt_bias=ssd_dt_bias[l], ssd_d=ssd_d[l],
                  ssd_norm_w=ssd_norm_w[l], pool_w=pool_w[l], pool_scale=pool_scale[l],
                  sconv_w=sconv_w[l], ret_decay_logit=ret_decay_logit[l], w_branch=w_branch[l],
                  w_gate=w_gate[l], b_gate=b_gate[l], w_o=w_o[l], ffn_up=ffn_up[l],
                  ffn_conv_w=ffn_conv_w[l], ffn_conv_b=ffn_conv_b[l], ffn_down=ffn_down[l])
        sh1, sc1, g1, sh2, sc2, g2 = jnp.split((jax.nn.silu(c) @ w_mod[l] + b_mod[l])[:, None, :], 6, axis=-1)
        csh1, csc1, cg1, csh2, csc2, cg2 = jnp.split(jax.nn.silu(c_ctx) @ w_mod[l] + b_mod[l], 6, axis=-1)
        hl = modulate(rmsnorm(xl, norm1_w[l]), sh1, sc1)
        hc = modulate(rmsnorm(xc, norm1_w[l]), csh1, csc1)
        fl = stream_features(hl, lp, rows)
        fc = stream_features(hc, lp, None)
        yc_ssd, hf_ssd, hb_ssd = bidir_scan(*fc['ssd'], ssd0, ssd0)
        yc_ret, hf_ret, hb_ret = bidir_scan(*fc['ret'], ret0, ret0)
        yl_ssd, _, _ = bidir_scan(*fl['ssd'], hf_ssd, hb_ssd)
        yl_ret, _, _ = bidir_scan(*fl['ret'], hf_ret, hb_ret)
        xl = xl + g1 * mixer_out(hl, fl, yl_ssd, yl_ret, lp)
        xl = xl + g2 * conv_ffn(modulate(rmsnorm(xl, norm2_w[l]), sh2, sc2), lp)
        if l < DEPTH - 1:
            xc = xc + cg1 * mixer_out(hc, fc, yc_ssd, yc_ret, lp)
            xc = xc + cg2 * conv_ffn(modulate(rmsnorm(xc, norm2_w[l]), csh2, csc2), lp)
    return rmsnorm(xl, final_norm_w)
```

```python
import contextlib
import numpy as np
import concourse.bass as bass
import concourse.mybir as mybir
from concourse.bass_utils import run_bass_kernel_spmd

F32 = mybir.dt.float32
BF16 = mybir.dt.bfloat16
AF = mybir.ActivationFunctionType
ALU = mybir.AluOpType
AX = mybir.AxisListType
PE, DVE, ACT, POOL, SP = "tensor", "vector", "scalar", "gpsimd", "sync"
ENGS = (PE, DVE, ACT, POOL, SP)

D = 2048
DC = 16
TC = 256
EPS = 1e-6
IN_COLS = 10256
DFF = 5632
O_Z, O_XBC, O_DT, O_POOL, O_SCB, O_SCC, O_SCX, O_RQ, O_RK, O_RV, O_RG = (
    0, 1024, 3072, 3088, 4112, 5136, 6160, 7184, 7696, 8208, 9232)
C_N1, C_N2, C_BM, C_CW, C_CB, C_PS, C_SW, C_BG, C_FW, C_FB, NCOL = 0, 16, 32, 128, 176, 192, 200, 224, 288, 552, 640
R_DB, R_AL, R_SD, R_RL, R_NW, NROW = 0, 32, 64, 80, 96, 1120
K_ID, K_ONE, K_TU, K_TL, K_SU, K_SL, K_DP, K_DN, K_I1, K_IR, K_PJ127, K_PJ, K_PINV, K_HM, NCST = (
    0, 128, 256, 384, 512, 640, 768, 896, 1024, 1152, 1280, 1281, 1282, 1346, 1348)
POOL_WINDOWS = (2, 4, 8, 16)


class Prog:
    def __init__(self, nc):
        self.nc = nc
        self.ops = []
        self.last_writer = {}
        self.readers = {}
        self.stack = contextlib.ExitStack()
        self.last_eng = {}
        self.dma_since = {}
        self.epoch = 0

    def add(self, eng, fn, reads=(), writes=(), dma=False, semkey=None, extra=()):
        idx = len(self.ops)
        deps = set()
        for r in reads:
            lw = self.last_writer.get(r)
            if lw is not None:
                deps.add((lw, "raw"))
        for w in writes:
            lw = self.last_writer.get(w)
            if lw is not None:
                deps.add((lw, "waw"))
            for rd in self.readers.get(w, ()):
                deps.add((rd, "war"))
        for d in extra:
            deps.add((d, "bar"))
        for r in reads:
            self.readers.setdefault(r, []).append(idx)
        for w in writes:
            self.last_writer[w] = idx
            self.readers[w] = []
        if dma:
            assert semkey is not None
            self.dma_since[semkey] = idx
        elif fn is not None:
            self.last_eng[eng] = idx
        self.ops.append(dict(eng=eng, fn=fn, deps=deps, dma=dma, semkey=semkey, signal=False, epoch=self.epoch))
        return idx

    def dma(self, eng, out, in_, reads=(), writes=(), semkey=None):
        return self.add(eng, lambda e: e.dma_start(out=out, in_=in_), reads, writes, dma=True, semkey=semkey)

    def barrier(self):
        tgt = list(self.last_eng.values()) + list(self.dma_since.values())
        for e in ENGS:
            self.add(e, None, extra=tgt)
        self.last_writer = {}
        self.readers = {}
        self.dma_since = {}
        self.epoch += 1

    def emit(self):
        ops, nc = self.ops, self.nc
        need = [None] * len(ops)
        for i, o in enumerate(ops):
            best = set()
            for (d, kind) in o["deps"]:
                p = ops[d]
                if not p["dma"] and not o["dma"] and p["eng"] == o["eng"]:
                    if o["eng"] == PE or kind != "raw":
                        continue
                best.add(d)
            need[i] = best
            for d in best:
                ops[d]["signal"] = True
        NPOOL = 90
        eng_sem = {e: self.stack.enter_context(nc.semaphore("se_" + e)) for e in (PE, DVE, ACT, POOL)}
        pool = [self.stack.enter_context(nc.semaphore("sd%d" % i)) for i in range(NPOOL)]
        free = {POOL: list(range(0, 30)), SP: list(range(30, NPOOL))}
        tot = [0] * NPOOL
        cur_map, cur_epoch, eng_cnt = {}, 0, {}
        for o in ops:
            if o["epoch"] != cur_epoch:
                for (q, k_), ph_ in cur_map.items():
                    free[q].append(ph_)
                cur_map, cur_epoch = {}, o["epoch"]
            if o["dma"]:
                k = (o["eng"], o["semkey"])
                if k not in cur_map:
                    assert free[o["eng"]], "out of DMA semaphores in one epoch"
                    cur_map[k] = free[o["eng"]].pop(0)
                ph = cur_map[k]
                tot[ph] += 16
                o["sem"], o["cnt"] = pool[ph], tot[ph]
            elif o["signal"]:
                e = o["eng"]
                eng_cnt[e] = eng_cnt.get(e, 0) + 1
                o["sem"], o["cnt"] = eng_sem[e], eng_cnt[e]
        self.n_sems = NPOOL + 4
        streams = {}
        for i, o in enumerate(ops):
            streams.setdefault(o["eng"], []).append(i)

        def make(engname, idxs):
            def body(e):
                waited = {}
                for i in idxs:
                    o = ops[i]
                    w = {}
                    for d in need[i]:
                        p = ops[d]
                        s, c = p["sem"], p["cnt"]
                        sid = id(s)
                        if waited.get(sid, 0) >= c:
                            continue
                        if sid not in w or w[sid][1] < c:
                            w[sid] = (s, c)
                    for sid, (s, c) in w.items():
                        e.wait_ge(s, c)
                        waited[sid] = c
                    if o["fn"] is None:
                        continue
                    ins = o["fn"](e)
                    if "sem" in o:
                        ins.then_inc(o["sem"], 16 if o["dma"] else 1)
            return body

        with nc.Block() as block:
            for engname, idxs in streams.items():
                getattr(block, engname)(make(engname, idxs))
        self.stats = {e: len(v) for e, v in streams.items()}


class Arena:
    def __init__(self, ap, words):
        self.ap, self.words, self.off, self.base = ap, words, 0, 0
        self.n = 0

    def mark(self):
        self.base = self.off

    def reset(self):
        self.off = self.base

    def alloc(self, shape, dtype, name=None):
        n = int(np.prod(shape))
        words = n if dtype == F32 else (n + 1) // 2
        words = (words + 7) // 8 * 8
        assert self.off + words <= self.words, "SBUF arena overflow: need %d have %d" % (self.off + words, self.words)
        v = self.ap[:, self.off:self.off + words]
        self.off += words
        if dtype != F32:
            v = v.bitcast(dtype)
        v = v[:, 0:n]
        if len(shape) == 2:
            v = v.rearrange("p (a b) -> p a b", a=shape[0])
        elif len(shape) == 3:
            v = v.rearrange("p (a b c) -> p a b c", a=shape[0], b=shape[1])
        self.n += 1
        return Buf(v, name or ("buf%d" % self.n))


class Buf:
    def __init__(self, ap, key):
        self.ap, self.key = ap, key

    def __getitem__(self, k):
        return self.ap[k]


class Ring:
    def __init__(self, bufs):
        self.bufs, self.i = bufs, 0

    def next(self):
        b = self.bufs[self.i % len(self.bufs)]
        self.i += 1
        return b


def build(NLC=32, L=4, debug=False, stop_after=None, E2L=99):
    TL = NLC * 128
    T = TC + TL
    NCH = 2 + NLC
    nc = bass.Bass("TRN2", target_bir_lowering=False)
    P = Prog(nc)

    def din(name, shape, dt=F32):
        return nc.dram_tensor(name, list(shape), dt, kind="ExternalInput").ap()

    def dscr(name, shape, dt):
        return nc.dram_tensor(name, list(shape), dt, kind=("ExternalOutput" if debug else "Internal")).ap()

    xT = din("xT", [D, TL]); cT = din("cT", [D, TC]); cvec = din("cvec", [128, DC, 2])
    w_mod = din("w_mod", [L, D, 6 * D]); w_in = din("w_in", [L, D, IN_COLS])
    pool_w = din("pool_w", [L, 4, 256, 256]); w_branch = din("w_branch", [L, 4, 1024, D])
    w_gate = din("w_gate", [L, 4, D, D]); w_o = din("w_o", [L, D, D])
    ffn_up = din("ffn_up", [L, D, 2 * DFF]); ffn_down = din("ffn_down", [L, DFF, D])
    colp = din("colp", [L, 128, NCOL]); rowp = din("rowp", [L, 128, NROW]); fin = din("fin", [128, DC])
    cst = din("cst", [128, NCST]); rope = din("rope", [TL, 128])
    outT = nc.dram_tensor("outT", [D, TL], F32, kind="ExternalOutput").ap()

    xres = dscr("xres", [D, T], F32)
    xmid = dscr("xmid", [D, T], F32)
    h1d = dscr("h1d", [D, T], BF16)
    X_tm = dscr("X_tm", [T, 1024], BF16); B_tm = dscr("B_tm", [T, 512], BF16)
    Zs_tm = dscr("Zs_tm", [T, 1024], BF16); K_tm = dscr("K_tm", [T, 512], BF16)
    V_tm = dscr("V_tm", [T, 1024], BF16); G_tm = dscr("G_tm", [T, 1024], BF16)
    DTL = dscr("DTL", [T, 64], F32)
    B_fm = dscr("B_fm", [512, T], BF16); C_fm = dscr("C_fm", [512, T], BF16)
    Q_fm = dscr("Q_fm", [512, T], BF16); K_fm = dscr("K_fm", [512, T], BF16)
    YP_fm = dscr("YP_fm", [1024, T], BF16); YC_fm = dscr("YC_fm", [1024, T], BF16)
    YS_fm = dscr("YS_fm", [1024, T], BF16); YR_fm = dscr("YR_fm", [1024, T], BF16)
    HFs = dscr("HFs", [NCH, 128, 1024], BF16); HFr = dscr("HFr", [NCH, 128, 512], BF16)
    dbgF = dscr("dbgF", [128, 16384], F32) if debug else None
    dbgB = dscr("dbgB", [128, 16384], BF16) if debug else None
    dbg_n = [0]

    def dump(ap, key, col, width, bf=False):
        if not debug:
            return
        dbg_n[0] += 1
        flat = ap
        P.dma(SP, (dbgB if bf else dbgF)[:, col:col + width], flat, reads=[key], semkey=("dbg", dbg_n[0]))

    AW = 49 * 1024
    arena_t = P.stack.enter_context(nc.sbuf_tensor("arena", [128, AW], F32))
    AR = Arena(arena_t, AW)
    psA = [Buf(P.stack.enter_context(nc.psum_tensor("psA%d" % i, [128, 512], F32))[:, :], "psA%d" % i) for i in range(6)]
    psT = [Buf(P.stack.enter_context(nc.psum_tensor("psT%d" % i, [128, 1024], BF16))[:, :], "psT%d" % i) for i in range(2)]
    psA_ring, psT_ring = Ring(psA), Ring(psT)

    def tt(eng, out, in0, in1, op, r, w):
        P.add(eng, lambda e: e.tensor_tensor(out=out, in0=in0, in1=in1, op=op), r, w)

    def ts(eng, out, in0, s1, s2, op0, op1, r, w):
        if op1 is None:
            P.add(eng, lambda e: e.tensor_scalar(out, in0, s1, None, op0), r, w)
        else:
            P.add(eng, lambda e: e.tensor_scalar(out, in0, s1, s2, op0, op1), r, w)

    def stt(eng, out, in0, scalar, in1, op0, op1, r, w):
        P.add(eng, lambda e: e.scalar_tensor_tensor(out=out, in0=in0, scalar=scalar, in1=in1, op0=op0, op1=op1), r, w)

    def act(out, in_, func, r, w, bias=None, scale=None, accum=None):
        kw = {}
        if bias is not None:
            kw["bias"] = bias
        if scale is not None:
            kw["scale"] = scale
        if accum is not None:
            kw["accum_out"] = accum
        P.add(ACT, lambda e: e.activation(out=out, in_=in_, func=func, **kw), r, w)

    def cpy(eng, out, in_, r, w):
        P.add(eng, lambda e: e.tensor_copy(out, in_), r, w)

    def mset(eng, out, val, r, w):
        P.add(eng, lambda e: e.memset(out, val), r, w)

    evac_flip = [0]

    def evac(out, in_, r, w):
        evac_flip[0] ^= 1
        if evac_flip[0]:
            act(out, in_, AF.Identity, r, w)
        else:
            cpy(DVE, out, in_, r, w)

    def bc(ap, shape):
        return ap.broadcast_to(list(shape))

    cst_sb = AR.alloc([NCST], F32, "cst")
    ident_bf = AR.alloc([128], BF16, "ident_bf")
    ones_bf = AR.alloc([128], BF16, "ones_bf")
    sil_bf = AR.alloc([DC, 2], BF16, "sil_bf")
    colp_sb = [AR.alloc([NCOL], F32, "colp%d" % i) for i in range(2)]
    rowp_sb = [AR.alloc([NROW], F32, "rowp%d" % i) for i in range(2)]
    modv = [AR.alloc([96, 2], F32, "modv%d" % i) for i in range(2)]
    A1 = [AR.alloc([DC, 2], F32, "A1_%d" % i) for i in range(2)]
    A2 = [AR.alloc([DC, 2], F32, "A2_%d" % i) for i in range(2)]
    poolw_sb = AR.alloc([4, 2, 256], BF16, "poolw")
    Aneg = AR.alloc([32], F32, "Aneg")
    lg = AR.alloc([16], F32, "lg")
    lgsel = AR.alloc([2, 4], F32, "lgsel")
    DtabR = AR.alloc([8, 128], F32, "DtabR")
    Gfb = AR.alloc([2, 4, 128], F32, "Gfb")
    Wfb = AR.alloc([2, 8], F32, "Wfb")
    CDsel = AR.alloc([2, 4], F32, "CDsel")
    eps_c = AR.alloc([4], F32, "eps_c")
    eps_ap = eps_c.ap[:, 0:1]
    one_ap = cst_sb.ap[:, K_ONE:K_ONE + 1]
    AR.mark()

    def C(name, n=128):
        off = {"id": K_ID, "one": K_ONE, "tu": K_TU, "tl": K_TL, "su": K_SU, "sl": K_SL, "dp": K_DP,
               "dn": K_DN, "i1": K_I1, "ir": K_IR}[name]
        return cst_sb.ap[:, off:off + n]

    P.dma(SP, cst_sb.ap, cst[:, :], writes=["cst"], semkey="cst")
    P.dma(POOL, ident_bf.ap, cst[:, K_ID:K_ID + 128], writes=["ident_bf"], semkey="idbf")
    P.dma(POOL, ones_bf.ap, cst[:, K_ONE:K_ONE + 128], writes=["ones_bf"], semkey="onbf")
    mset(POOL, eps_c.ap, EPS, [], ["eps_c"])
    cv = AR.alloc([DC, 2], F32, "cv")
    P.dma(SP, cv.ap, cvec[:, :, :], writes=["cv"], semkey="cv")
    act(sil_bf.ap, cv.ap, AF.Silu, ["cv"], ["sil_bf"])
    xres_v = xres.rearrange("(c p) t -> p c t", p=128)
    xmid_v = xmid.rearrange("(c p) t -> p c t", p=128)
    for c in range(DC):
        P.dma(SP, xres[c * 128:(c + 1) * 128, 0:TC], cT[c * 128:(c + 1) * 128, :], writes=[("xres", c, "c")], semkey=("xinit", c % 4))
        P.dma(SP, xres[c * 128:(c + 1) * 128, TC:T], xT[c * 128:(c + 1) * 128, :], writes=[("xres", c, "l")], semkey=("xinit", c % 4))
    P.barrier()
    AR.reset()

    streams = [(0, TC), (TC, T)]

    def blocks_of(s0, s1, tb=1024):
        return [(t, min(tb, s1 - t)) for t in range(s0, s1, tb)]

    state = dict(P=P, nc=nc)
    for l in range(L):
        par = l % 2
        cp, rp, mv = colp_sb[par], rowp_sb[par], modv[par]
        last = (l == L - 1)

        P.dma(SP, cp.ap, colp[l, :, :], writes=[cp.key], semkey=cp.key)
        P.dma(SP, rp.ap, rowp[l, :, :], writes=[rp.key], semkey=rp.key)
        P.dma(POOL, poolw_sb.ap, pool_w[l].rearrange("g (cc p) d -> p g cc d", p=128), writes=["poolw"], semkey="poolw")
        wm_ring = Ring([AR.alloc([DC, 512], BF16, "wm%d" % i) for i in range(2)])
        psm = psA_ring.next()
        for jb in range(24):
            wt = wm_ring.next()
            P.dma(POOL, wt.ap, w_mod[l, :, jb * 512:(jb + 1) * 512].rearrange("(k p) m -> p k m", p=128),
                  writes=[wt.key], semkey=wt.key)

            def f(e, wt=wt, jb=jb, psm=psm):
                for jj in range(4):
                    j = jb * 4 + jj
                    for k in range(DC):
                        ins = e.matmul(psm.ap[:, 2 * j:2 * j + 2], wt.ap[:, k, jj * 128:(jj + 1) * 128], sil_bf.ap[:, k, :],
                                       start=(k == 0), stop=(k == DC - 1))
                return ins
            P.add(PE, f, [wt.key, "sil_bf"], [psm.key])
        tt(DVE, mv.ap, psm.ap[:, 0:192].rearrange("p (j s) -> p j s", s=2),
           bc(cp.ap[:, C_BM:C_BM + 96].unsqueeze(2), [128, 96, 2]), ALU.add, [psm.key, cp.key], [mv.key])
        for (Ax, wcol, scj) in ((A1[par], C_N1, 16), (A2[par], C_N2, 64)):
            ts(DVE, Ax.ap, mv.ap[:, scj:scj + 16, :], 1.0, None, ALU.add, None, [mv.key], [Ax.key])
            tt(DVE, Ax.ap, Ax.ap, bc(cp.ap[:, wcol:wcol + 16].unsqueeze(2), [128, 16, 2]), ALU.mult, [Ax.key, cp.key], [Ax.key])
        tmp32 = AR.alloc([32], F32, "tmp32")
        act(tmp32.ap, rp.ap[:, R_AL:R_AL + 32], AF.Exp, [rp.key], ["tmp32"])
        ts(DVE, Aneg.ap, tmp32.ap, -1.0, None, ALU.mult, None, ["tmp32"], ["Aneg"])
        tmp16 = AR.alloc([16], F32, "tmp16")
        act(tmp16.ap, rp.ap[:, R_RL:R_RL + 16], AF.Exp, [rp.key], ["tmp16"], scale=-1.0)
        act(tmp16.ap, tmp16.ap, AF.Ln, ["tmp16", "cst"], ["tmp16"], bias=one_ap)
        ts(DVE, lg.ap, tmp16.ap, -1.0, None, ALU.mult, None, ["tmp16"], ["lg"])
        lgv = lg.ap.rearrange("p (d q two) -> p d q two", d=2, two=2)
        cpy(DVE, lgsel.ap[0:64], lgv[0:64, :, :, 0], ["lg"], ["lgsel"])
        cpy(DVE, lgsel.ap[64:128], lgv[64:128, :, :, 1], ["lg", "lgsel"], ["lgsel"])
        tD = AR.alloc([8, 128], F32, "tD")
        tt(DVE, DtabR.ap, bc(C("dp").unsqueeze(1), [128, 8, 128]), bc(lg.ap[:, 0:8].unsqueeze(2), [128, 8, 128]), ALU.mult, ["cst", "lg"], ["DtabR"])
        tt(DVE, tD.ap, bc(C("dn").unsqueeze(1), [128, 8, 128]), bc(lg.ap[:, 8:16].unsqueeze(2), [128, 8, 128]), ALU.mult, ["cst", "lg"], ["tD"])
        tt(DVE, DtabR.ap, DtabR.ap, tD.ap, ALU.add, ["DtabR", "tD"], ["DtabR"])
        act(DtabR.ap, DtabR.ap, AF.Exp, ["DtabR"], ["DtabR"])
        tt(DVE, DtabR.ap, DtabR.ap, bc(C("id").unsqueeze(1), [128, 8, 128]), ALU.add, ["DtabR", "cst"], ["DtabR"])
        for d, io in ((0, "i1"), (1, "ir")):
            tt(DVE, Gfb.ap[:, d], bc(lgsel.ap[:, d, :].unsqueeze(2), [128, 4, 128]), bc(C(io).unsqueeze(1), [128, 4, 128]), ALU.mult, ["lgsel", "cst"], ["Gfb"])
        act(Gfb.ap, Gfb.ap, AF.Exp, ["Gfb"], ["Gfb"])
        ts(DVE, Wfb.ap[:, 0, :], lg.ap[:, 0:8], cst_sb.ap[:, K_PJ127:K_PJ127 + 1], None, ALU.mult, None, ["lg", "cst"], ["Wfb"])
        ts(DVE, Wfb.ap[:, 1, :], lg.ap[:, 8:16], cst_sb.ap[:, K_PJ:K_PJ + 1], None, ALU.mult, None, ["lg", "cst", "Wfb"], ["Wfb"])
        act(Wfb.ap, Wfb.ap, AF.Exp, ["Wfb"], ["Wfb"])
        act(CDsel.ap, lgsel.ap, AF.Exp, ["lgsel"], ["CDsel"], scale=128.0)
        P.barrier()
        AR.reset()
        if stop_after == "A":
            break

        build_mixer_in(locals())
        if stop_after == "C":
            break
        build_scans(locals())
        if stop_after in ("E", "E1"):
            break
        build_merge(locals())
        if stop_after == "F":
            break
        build_ffn(locals())

    if stop_after is None:
        build_final(locals())
    P.barrier()
    P.emit()
    return nc, P


class NS:
    def __init__(self, d):
        self.__dict__.update(d)


def norm_block(n, hext, src_v, t0, Tb, s0, s1, Abuf, Bview, Bkey, si):
    P, AR = n.P, n.AR
    xt_ring = Ring([AR.alloc([DC, 256], F32, "xt%d" % i) for i in range(2)])
    sq_ring = Ring([AR.alloc([DC, 256], BF16, "sq%d" % i) for i in range(2)])
    rs_ring = Ring([AR.alloc([256], F32, "rs%d" % i) for i in range(2)])
    hn_ring = Ring([AR.alloc([256], F32, "hn%d" % i) for i in range(3)])
    subs = []
    if t0 > s0:
        subs.append(("L", t0 - 8, 8, 0))
    for j in range(0, Tb, 256):
        subs.append((j // 256, t0 + j, min(256, Tb - j), 8 + j))
    if t0 + Tb < s1:
        subs.append(("R", t0 + Tb, 8, 8 + Tb))
    for (sid, ta, w, col) in subs:
        xt, sq, rs = xt_ring.next(), sq_ring.next(), rs_ring.next()
        P.dma(SP, xt.ap[:, :, 0:w], src_v[:, :, ta:ta + w], writes=[xt.key], semkey=xt.key)
        n.act(sq.ap[:, :, 0:w], xt.ap[:, :, 0:w], AF.Square, [xt.key], [sq.key])
        ps = n.psA_ring.next()

        def f(e, ps=ps, sq=sq, w=w):
            for k in range(DC):
                ins = e.matmul(ps.ap[:, 0:w], n.ones_bf.ap, sq.ap[:, k, 0:w], start=(k == 0), stop=(k == DC - 1))
            return ins
        P.add(PE, f, [sq.key, "ones_bf"], [ps.key])
        n.act(rs.ap[:, 0:w], ps.ap[:, 0:w], AF.Ln, [ps.key], [rs.key], bias=n.eps_ap, scale=1.0 / D)
        n.act(rs.ap[:, 0:w], rs.ap[:, 0:w], AF.Exp, [rs.key], [rs.key], scale=-0.5)
        for k in range(DC):
            hn = hn_ring.next()
            n.stt(DVE, hn.ap[:, 0:w], xt.ap[:, k, 0:w], Abuf.ap[:, k, si:si + 1], rs.ap[:, 0:w], ALU.mult, ALU.mult,
                  [xt.key, rs.key, Abuf.key], [hn.key])
            n.act(hext.ap[:, k, col:col + w], hn.ap[:, 0:w], AF.Identity, [hn.key, Bkey], [("hext", sid)],
                  bias=Bview[:, k, si:si + 1])


def hext_keys(Tb, c0, c1):
    return [("hext", j) for j in range(c0 // 256, (c1 - 1) // 256 + 1)]


def fm_gemm(n, hext, Tb, haloL, haloR, wslot, wc0, raw, KC=DC):
    P = n.P
    for ti in range((Tb + 511) // 512):
        w = min(512, Tb - ti * 512)
        ps = n.psA_ring.next()

        def f(e, ps=ps, ti=ti, w=w):
            for k in range(KC):
                ins = e.matmul(ps.ap[:, 0:w], wslot.ap[:, k, wc0:wc0 + 128], hext.ap[:, k, 8 + ti * 512:8 + ti * 512 + w],
                               start=(k == 0), stop=(k == KC - 1))
            return ins
        P.add(PE, f, [wslot.key] + hext_keys(Tb, ti * 512, ti * 512 + w), [ps.key])
        n.evac(raw.ap[:, 8 + ti * 512:8 + ti * 512 + w], ps.ap[:, 0:w], [ps.key], [(raw.key, ti)])
    if haloL or haloR:
        psh = n.psA_ring.next()

        def fh(e, psh=psh):
            ins = None
            for (flag, pc, hc) in ((haloL, 0, 0), (haloR, 8, 8 + Tb)):
                if flag:
                    for k in range(KC):
                        ins = e.matmul(psh.ap[:, pc:pc + 8], wslot.ap[:, k, wc0:wc0 + 128], hext.ap[:, k, hc:hc + 8],
                                       start=(k == 0), stop=(k == KC - 1))
            return ins
        P.add(PE, fh, [wslot.key] + ([("hext", "L")] if haloL else []) + ([("hext", "R")] if haloR else []), [psh.key])
    for (flag, pc, hc, nm) in ((haloL, 0, 0, "L"), (haloR, 8, 8 + Tb, "R")):
        if flag:
            n.cpy(DVE, raw.ap[:, hc:hc + 8], psh.ap[:, pc:pc + 8], [psh.key], [(raw.key, nm)])
        else:
            n.mset(POOL, raw.ap[:, hc:hc + 8], 0.0, [], [(raw.key, nm)])


def raw_keys(raw, Tb):
    return [(raw.key, ti) for ti in range((Tb + 511) // 512)] + [(raw.key, "L"), (raw.key, "R")]


def conv3(n, raw, rk, Tb, w0, w1, w2, bias, tmp_ring, wkey):
    r0, r1 = tmp_ring.next(), tmp_ring.next()
    if bias is None:
        n.act(r0.ap[:, 0:Tb], raw.ap[:, 8:8 + Tb], AF.Identity, rk + [wkey], [r0.key], scale=w1)
    else:
        n.act(r0.ap[:, 0:Tb], raw.ap[:, 8:8 + Tb], AF.Identity, rk + [wkey], [r0.key], scale=w1, bias=bias)
    n.stt(DVE, r1.ap[:, 0:Tb], raw.ap[:, 7:7 + Tb], w0, r0.ap[:, 0:Tb], ALU.mult, ALU.add, rk + [wkey, r0.key], [r1.key])
    n.stt(DVE, r0.ap[:, 0:Tb], raw.ap[:, 9:9 + Tb], w2, r1.ap[:, 0:Tb], ALU.mult, ALU.add, rk + [wkey, r1.key], [r0.key])
    return r0


def transposes_to(n, src, ncols_blocks, src_key, dst_view, dst_key):
    P = n.P
    pT = n.psT_ring.next()

    def f(e, pT=pT):
        for b in range(ncols_blocks):
            ins = e.transpose(pT.ap[:, b * 128:(b + 1) * 128], src[:, b * 128:(b + 1) * 128], n.ident_bf.ap)
        return ins
    P.add(PE, f, [src_key, "ident_bf"], [pT.key])
    n.evac(dst_view, pT.ap[:, 0:ncols_blocks * 128].rearrange("p (b f) -> p b f", f=128), [pT.key], [dst_key])


def build_mixer_in(g):
    n = NS(g)
    P, AR, l, cp, rp, mv = n.P, n.AR, n.l, n.cp, n.rp, n.mv
    Win = n.w_in[l]
    h1d_v = n.h1d.rearrange("(c p) t -> p c t", p=128)
    for si, (s0, s1) in enumerate(n.streams):
        for (t0, Tb) in n.blocks_of(s0, s1):
            W = Tb + 16
            ntc = Tb // 128
            haloL, haloR = t0 > s0, t0 + Tb < s1
            lat = si == 1
            base0 = AR.off
            hext = AR.alloc([DC, W], BF16, "hext")
            base1 = AR.off
            norm_block(n, hext, n.xres_v, t0, Tb, s0, s1, n.A1[n.par], mv.ap[:, 0:16, :], mv.key, si)
            P.dma(SP, h1d_v[:, :, t0:t0 + Tb], hext.ap[:, :, 8:8 + Tb], reads=hext_keys(Tb, 0, Tb), semkey="h1st")
            P.barrier()
            AR.off = base1
            wfm = Ring([AR.alloc([DC, 256], BF16, "wfm%d" % i) for i in range(4)])
            raws = Ring([AR.alloc([W], F32, "raw%d" % i) for i in range(6)])
            tmps = Ring([AR.alloc([W], F32, "tmp%d" % i) for i in range(5)])
            obfs = Ring([AR.alloc([Tb], BF16, "obf%d" % i) for i in range(4)])
            xtm = AR.alloc([ntc, 1024], BF16, "xtm")
            btm = AR.alloc([ntc, 512], BF16, "btm")

            def wload(col):
                ws = wfm.next()
                P.dma(POOL, ws.ap, Win[:, col:col + 256].rearrange("(k p) m -> p k m", p=128), writes=[ws.key], semkey=ws.key)
                return ws
            for pr in range(8):
                ws = wload(O_XBC + pr * 256)
                for cc in range(2):
                    i = pr * 2 + cc
                    raw = raws.next()
                    fm_gemm(n, hext, Tb, haloL, haloR, ws, cc * 128, raw)
                    r = conv3(n, raw, raw_keys(raw, Tb), Tb, cp.ap[:, C_CW + i:C_CW + i + 1], cp.ap[:, C_CW + 16 + i:C_CW + 17 + i],
                              cp.ap[:, C_CW + 32 + i:C_CW + 33 + i], cp.ap[:, C_CB + i:C_CB + i + 1], tmps, cp.key)
                    ob = obfs.next()
                    n.act(ob.ap[:, 0:Tb], r.ap[:, 0:Tb], AF.Silu, [r.key], [ob.key])
                    if i < 8:
                        transposes_to(n, ob.ap, ntc, ob.key, xtm.ap[:, :, i * 128:(i + 1) * 128], ("xtm", i))
                    elif i < 12:
                        P.dma(SP, n.B_fm[(i - 8) * 128:(i - 7) * 128, t0:t0 + Tb], ob.ap[:, 0:Tb], reads=[ob.key], semkey=("st", ob.key))
                        transposes_to(n, ob.ap, ntc, ob.key, btm.ap[:, :, (i - 8) * 128:(i - 7) * 128], ("btm", i))
                    else:
                        P.dma(SP, n.C_fm[(i - 12) * 128:(i - 11) * 128, t0:t0 + Tb], ob.ap[:, 0:Tb], reads=[ob.key], semkey=("st", ob.key))
            P.dma(SP, n.X_tm[t0:t0 + Tb, :].rearrange("(c p) f -> p c f", p=128), xtm.ap, reads=[("xtm", i) for i in range(8)], semkey="xtm")
            P.dma(SP, n.B_tm[t0:t0 + Tb, :].rearrange("(c p) f -> p c f", p=128), btm.ap, reads=[("btm", i) for i in range(8, 12)], semkey="btm")
            for gq in range(4):
                ws = wload(O_POOL + gq * 256)
                win = POOL_WINDOWS[gq]
                half = win // 2
                pls = []
                for cc in range(2):
                    raw = raws.next()
                    fm_gemm(n, hext, Tb, haloL, haloR, ws, cc * 128, raw)
                    rk = raw_keys(raw, Tb)
                    cur, curk, ln_, step = raw, rk, W, 1
                    while step * 2 < win:
                        nxt = tmps.next()
                        ln2 = ln_ - step
                        n.tt(POOL, nxt.ap[:, 0:ln2], cur.ap[:, 0:ln2], cur.ap[:, step:step + ln2], ALU.add, curk, [nxt.key])
                        cur, curk, ln_, step = nxt, [nxt.key], ln2, step * 2
                    S = tmps.next()
                    n.tt(POOL, S.ap[:, 0:Tb], cur.ap[:, 8 - half:8 - half + Tb], cur.ap[:, 8 - half + step:8 - half + step + Tb], ALU.add, curk, [S.key])
                    pl = obfs.next()
                    n.stt(DVE, pl.ap[:, 0:Tb], S.ap[:, 0:Tb], 1.0 / win, raw.ap[:, 8:8 + Tb], ALU.mult, ALU.subtract, [S.key] + rk, [pl.key])
                    if not haloL:
                        e_ = tmps.next()
                        pc = K_PINV + (gq * 2) * 8
                        n.tt(DVE, e_.ap[:, 0:half], S.ap[:, 0:half], n.cst_sb.ap[:, pc:pc + half], ALU.mult, [S.key, "cst"], [e_.key])
                        n.tt(DVE, pl.ap[:, 0:half], e_.ap[:, 0:half], raw.ap[:, 8:8 + half], ALU.subtract, [e_.key, pl.key] + rk, [pl.key])
                    if not haloR and half > 1:
                        e_ = tmps.next()
                        pc = K_PINV + (gq * 2 + 1) * 8
                        m = half - 1
                        n.tt(DVE, e_.ap[:, 0:m], S.ap[:, Tb - m:Tb], n.cst_sb.ap[:, pc:pc + m], ALU.mult, [S.key, "cst"], [e_.key])
                        n.tt(DVE, pl.ap[:, Tb - m:Tb], e_.ap[:, 0:m], raw.ap[:, 8 + Tb - m:8 + Tb], ALU.subtract, [e_.key, pl.key] + rk, [pl.key])
                    pls.append(pl)
                for dch in range(2):
                    yp = obfs.next()
                    for ti in range((Tb + 511) // 512):
                        w = min(512, Tb - ti * 512)
                        ps = n.psA_ring.next()

                        def f(e, ps=ps, ti=ti, w=w, dch=dch, gq=gq, pls=pls):
                            for cc in range(2):
                                ins = e.matmul(ps.ap[:, 0:w], n.poolw_sb.ap[:, gq, cc, dch * 128:(dch + 1) * 128],
                                               pls[cc].ap[:, ti * 512:ti * 512 + w], start=(cc == 0), stop=(cc == 1))
                            return ins
                        P.add(PE, f, ["poolw", pls[0].key, pls[1].key], [ps.key])
                        col = C_PS + gq * 2 + dch
                        n.act(yp.ap[:, ti * 512:ti * 512 + w], ps.ap[:, 0:w], AF.Identity, [ps.key, cp.key], [yp.key], scale=cp.ap[:, col:col + 1])
                    r0_ = gq * 256 + dch * 128
                    P.dma(SP, n.YP_fm[r0_:r0_ + 128, t0:t0 + Tb], yp.ap[:, 0:Tb], reads=[yp.key], semkey=("st", yp.key))
            for pr in range(4):
                wsb, wsc, wsx = wload(O_SCB + pr * 256), wload(O_SCC + pr * 256), wload(O_SCX + pr * 256)
                for cc in range(2):
                    i = pr * 2 + cc
                    rb, rc, rx = raws.next(), raws.next(), raws.next()
                    fm_gemm(n, hext, Tb, haloL, haloR, wsb, cc * 128, rb)
                    fm_gemm(n, hext, Tb, haloL, haloR, wsc, cc * 128, rc)
                    fm_gemm(n, hext, Tb, haloL, haloR, wsx, cc * 128, rx)
                    pb = tmps.next()
                    n.tt(POOL, pb.ap[:, 7:9 + Tb], rc.ap[:, 7:9 + Tb], rx.ap[:, 7:9 + Tb], ALU.mult, raw_keys(rc, Tb) + raw_keys(rx, Tb), [pb.key])
                    r = conv3(n, pb, [pb.key], Tb, cp.ap[:, C_SW + i:C_SW + i + 1], cp.ap[:, C_SW + 8 + i:C_SW + 9 + i],
                              cp.ap[:, C_SW + 16 + i:C_SW + 17 + i], None, tmps, cp.key)
                    ob = obfs.next()
                    n.tt(DVE, ob.ap[:, 0:Tb], r.ap[:, 0:Tb], rb.ap[:, 8:8 + Tb], ALU.mult, [r.key] + raw_keys(rb, Tb), [ob.key])
                    P.dma(SP, n.YC_fm[i * 128:(i + 1) * 128, t0:t0 + Tb], ob.ap[:, 0:Tb], reads=[ob.key], semkey=("st", ob.key))
            P.barrier()
            AR.off = base1
            wtm = Ring([AR.alloc([DC, 512], BF16, "wtm%d" % i) for i in range(2)])
            tms = Ring([AR.alloc([512], BF16, "tms%d" % i) for i in range(4)])
            ropes = Ring([AR.alloc([128], F32, "rope%d" % i) for i in range(2)])
            ras = Ring([AR.alloc([512], F32, "ra%d" % i) for i in range(2)])
            rbs = Ring([AR.alloc([512], F32, "rb%d" % i) for i in range(2)])
            qkst = Ring([AR.alloc([4, Tb], BF16, "qkst%d" % i) for i in range(2)])
            dtx = Ring([AR.alloc([32], F32, "dtx%d" % i) for i in range(2)])
            dtls = Ring([AR.alloc([64], F32, "dtl%d" % i) for i in range(2)])
            jobs = [("z", O_Z, 512, 0), ("z", O_Z + 512, 512, 1), ("dt", O_DT, 16, 0), ("q", O_RQ, 512, 0), ("k", O_RK, 512, 0),
                    ("v", O_RV, 512, 0), ("v", O_RV + 512, 512, 1), ("g", O_RG, 512, 0), ("g", O_RG + 512, 512, 1)]
            for (kind, col, ncl, blk) in jobs:
                wt = wtm.next()
                P.dma(POOL, wt.ap[:, :, 0:ncl], Win[:, col:col + ncl].rearrange("(k p) m -> p k m", p=128), writes=[wt.key], semkey=wt.key)
                stage = qkst.next() if kind in ("q", "k") else None
                for tc in range(ntc):
                    ps = n.psA_ring.next()

                    def f(e, ps=ps, wt=wt, tc=tc, ncl=ncl, hext=hext):
                        for k in range(DC):
                            ins = e.matmul(ps.ap[:, 0:ncl], hext.ap[:, k, 8 + tc * 128:8 + (tc + 1) * 128], wt.ap[:, k, 0:ncl],
                                           start=(k == 0), stop=(k == DC - 1))
                        return ins
                    P.add(PE, f, [wt.key, ("hext", tc // 2)], [ps.key])
                    rows = slice(t0 + tc * 128, t0 + (tc + 1) * 128)
                    if kind in ("z", "g", "v"):
                        st = tms.next()
                        if kind == "v":
                            n.evac(st.ap, ps.ap, [ps.key], [st.key])
                        else:
                            n.act(st.ap, ps.ap, AF.Silu, [ps.key], [st.key])
                        dst = {"z": n.Zs_tm, "g": n.G_tm, "v": n.V_tm}[kind]
                        P.dma(SP, dst[rows, blk * 512:(blk + 1) * 512], st.ap, reads=[st.key], semkey=("st", st.key))
                    elif kind == "dt":
                        xd, dtl = dtx.next(), dtls.next()
                        n.tt(DVE, xd.ap.rearrange("p (d h) -> p d h", d=2), n.bc(ps.ap[:, 0:16].unsqueeze(1), [128, 2, 16]),
                             rp.ap[:, R_DB:R_DB + 32].rearrange("p (d h) -> p d h", d=2), ALU.add, [ps.key, rp.key], [xd.key])
                        n.ts(DVE, xd.ap, xd.ap, 60.0, None, ALU.min, None, [xd.key], [xd.key])
                        n.act(xd.ap, xd.ap, AF.Exp, [xd.key], [xd.key])
                        n.act(dtl.ap[:, 0:32], xd.ap, AF.Ln, [xd.key, "cst"], [dtl.key], bias=n.one_ap)
                        n.tt(DVE, dtl.ap[:, 32:64], dtl.ap[:, 0:32], n.Aneg.ap, ALU.mult, [dtl.key, "Aneg"], [dtl.key])
                        P.dma(SP, n.DTL[rows, :], dtl.ap, reads=[dtl.key], semkey=("st", dtl.key))
                    else:
                        qs = 0.125 if kind == "q" else 1.0
                        qt = tms.next()
                        if lat:
                            rt = ropes.next()
                            P.dma(SP, rt.ap, n.rope[t0 - TC + tc * 128:t0 - TC + (tc + 1) * 128, :], writes=[rt.key], semkey=rt.key)
                            A_, B_ = ras.next(), rbs.next()
                            n.stt(DVE, A_.ap.rearrange("p (h d) -> p h d", h=8), ps.ap.rearrange("p (h d) -> p h d", h=8), qs,
                                  n.bc(rt.ap[:, 0:64].unsqueeze(1), [128, 8, 64]), ALU.mult, ALU.mult, [ps.key, rt.key], [A_.key])
                            ps5 = ps.ap.rearrange("p (h r x f) -> p h r x f", h=8, r=2, x=2)
                            B5 = B_.ap.rearrange("p (h r x f) -> p h r x f", h=8, r=2, x=2)
                            sn = rt.ap[:, 64:128].rearrange("p (r x f) -> p r x f", r=2, x=2)
                            for x in range(2):
                                for rc in range(2):
                                    n.stt(DVE, B5[:, :, rc, x, :], ps5[:, :, rc, 1 - x, :], qs, n.bc(sn[:, rc, x, :].unsqueeze(1), [128, 8, 16]),
                                          ALU.mult, ALU.mult, [ps.key, rt.key], [B_.key])
                            n.tt(POOL, qt.ap, A_.ap, B_.ap, ALU.add, [A_.key, B_.key], [qt.key])
                        else:
                            n.act(qt.ap, ps.ap, AF.Identity, [ps.key], [qt.key], scale=qs)
                        if kind == "k":
                            P.dma(SP, n.K_tm[rows, :], qt.ap, reads=[qt.key], semkey=("st", qt.key))
                        transposes_to(n, qt.ap, 4, qt.key, stage.ap[:, :, tc * 128:(tc + 1) * 128], (stage.key, tc))
                if stage is not None:
                    dst = n.Q_fm if kind == "q" else n.K_fm
                    P.dma(SP, dst.rearrange("(fc p) t -> p fc t", p=128)[:, :, t0:t0 + Tb], stage.ap,
                          reads=[(stage.key, tc) for tc in range(ntc)], semkey=("st", stage.key))
            P.barrier()
            AR.off = base0


def build_scans(g):
    n = NS(g)
    P, AR, rp, NCH = n.P, n.AR, n.rp, n.NCH
    tt, ts, stt, act, cpy, mset, bc, C = n.tt, n.ts, n.stt, n.act, n.cpy, n.mset, n.bc, n.C
    nxt = n.psA_ring.next
    base = AR.off
    one_f = C("one")

    def v3(ap, a):
        return ap.rearrange("p (a b) -> p a b", a=a)

    hS = AR.alloc([1024], F32, "hS")
    hR = AR.alloc([512], F32, "hR")
    mset(POOL, hS.ap, 0.0, [], ["hS"])
    mset(POOL, hR.ap, 0.0, [], ["hR"])
    lds = Ring([dict(dtl=AR.alloc([64], F32, "l1d%d" % i), x=AR.alloc([1024], BF16, "l1x%d" % i), b=AR.alloc([512], BF16, "l1b%d" % i),
                     k=AR.alloc([512], BF16, "l1k%d" % i), v=AR.alloc([1024], BF16, "l1v%d" % i)) for i in range(2)])
    hsb = Ring([AR.alloc([1024], BF16, "hsb%d" % i) for i in range(2)])
    hrb = Ring([AR.alloc([512], BF16, "hrb%d" % i) for i in range(2)])
    sms = Ring([AR.alloc([5, 16], F32, "sm%d" % i) for i in range(2)])
    vws = Ring([AR.alloc([1024], BF16, "vw%d" % i) for i in range(2)])
    kws = Ring([AR.alloc([512], BF16, "kw%d" % i) for i in range(2)])
    tSs = Ring([AR.alloc([1024], F32, "tS%d" % i) for i in range(2)])
    tRs = Ring([AR.alloc([512], F32, "tR%d" % i) for i in range(2)])
    for c in range(NCH):
        rows = slice(c * 128, (c + 1) * 128)
        ld = lds.next()
        for nm, src in (("dtl", n.DTL), ("x", n.X_tm), ("b", n.B_tm), ("k", n.K_tm), ("v", n.V_tm)):
            P.dma(SP, ld[nm].ap, src[rows, :], writes=[ld[nm].key], semkey=ld[nm].key)
        sb, rb_ = hsb.next(), hrb.next()
        act(sb.ap, hS.ap, AF.Identity, ["hS"], [sb.key])
        P.dma(SP, n.HFs[c], sb.ap, reads=[sb.key], semkey=("st", sb.key))
        cpy(POOL, rb_.ap, hR.ap, ["hR"], [rb_.key])
        P.dma(SP, n.HFr[c], rb_.ap, reads=[rb_.key], semkey=("st", rb_.key))
        dtl = ld["dtl"]
        psS = nxt()

        def f(e, psS=psS, dtl=dtl):
            e.matmul(psS.ap[:, 0:16], one_f, dtl.ap[:, 32:48], start=True, stop=True)
            return e.matmul(psS.ap[:, 16:32], C("tu"), dtl.ap[:, 32:48], start=True, stop=True)
        P.add(PE, f, ["cst", dtl.key], [psS.key])
        sm = sms.next()
        act(sm.ap[:, 0, :], psS.ap[:, 0:16], AF.Identity, [psS.key], [(sm.key, 0)])
        tt(DVE, sm.ap[:, 1, :], sm.ap[:, 0, :], psS.ap[:, 16:32], ALU.subtract, [(sm.key, 0), psS.key], [(sm.key, 1)])
        act(sm.ap[:, 2, :], sm.ap[:, 1, :], AF.Exp, [(sm.key, 1)], [(sm.key, 2)])
        act(sm.ap[:, 3, :], sm.ap[:, 0, :], AF.Exp, [(sm.key, 0)], [(sm.key, 3)])
        tt(DVE, sm.ap[:, 4, :], dtl.ap[:, 0:16], sm.ap[:, 2, :], ALU.mult, [dtl.key, (sm.key, 2)], [(sm.key, 4)])
        vw = vws.next()
        tt(DVE, v3(vw.ap, 16), v3(ld["x"].ap, 16), bc(sm.ap[:, 4, :].unsqueeze(2), [128, 16, 64]), ALU.mult, [ld["x"].key, (sm.key, 4)], [vw.key])
        pa, pb = nxt(), nxt()

        def f(e, pa=pa, pb=pb, ld=ld, vw=vw):
            for gq in range(4):
                ins = e.matmul((pa if gq < 2 else pb).ap[:, (gq % 2) * 256:(gq % 2) * 256 + 256], ld["b"].ap[:, gq * 128:(gq + 1) * 128],
                               vw.ap[:, gq * 256:(gq + 1) * 256], start=True, stop=True)
            return ins
        P.add(PE, f, [ld["b"].key, vw.key], [pa.key, pb.key])
        tS = tSs.next()
        tt(POOL, v3(tS.ap, 16), v3(hS.ap, 16), bc(sm.ap[:, 3, :].unsqueeze(2), [128, 16, 64]), ALU.mult, ["hS", (sm.key, 3)], [tS.key])
        tt(DVE, hS.ap[:, 0:512], tS.ap[:, 0:512], pa.ap, ALU.add, [tS.key, pa.key], ["hS"])
        tt(DVE, hS.ap[:, 512:1024], tS.ap[:, 512:1024], pb.ap, ALU.add, [tS.key, pb.key, "hS"], ["hS"])
        kw = kws.next()
        tt(DVE, v3(kw.ap, 8), v3(ld["k"].ap, 8), bc(n.Wfb.ap[:, 0, :].unsqueeze(2), [128, 8, 64]), ALU.mult, [ld["k"].key, "Wfb"], [kw.key])
        ret_state_update(n, kw, ld["v"], hR, "hR", 0, tRs)
    P.barrier()
    AR.off = base
    if n.stop_after == "E1":
        return

    gS = AR.alloc([1024], F32, "gS")
    gR = AR.alloc([512], F32, "gR")
    gSb = AR.alloc([1024], BF16, "gSb")
    gRb = AR.alloc([512], BF16, "gRb")
    for b_ in (gS, gR, gSb, gRb):
        mset(POOL, b_.ap, 0.0, [], [b_.key])
    spec = (("dtl", [64], F32), ("x", [1024], BF16), ("b", [512], BF16), ("bf", [4, 128], BF16), ("cf", [4, 128], BF16),
            ("z", [1024], BF16), ("k", [512], BF16), ("kf", [4, 128], BF16), ("qf", [4, 128], BF16), ("v", [1024], BF16),
            ("g", [1024], BF16), ("hs", [1024], BF16), ("hr", [512], BF16))
    lds = Ring([{nm: AR.alloc(sh, dt, "l2%s%d" % (nm, i)) for (nm, sh, dt) in spec} for i in range(2)])
    sms = Ring([AR.alloc([6, 16], F32, "sn%d" % i) for i in range(2)])
    Rs = Ring([AR.alloc([16, 128], F32, "R%d" % i) for i in range(2)])
    Es = Ring([AR.alloc([16, 128], F32, "E%d" % i) for i in range(2)])
    scms = Ring([AR.alloc([4, 128], F32, "scm%d" % i) for i in range(2)])
    Ms = Ring([AR.alloc([16, 128], BF16, "M%d" % i) for i in range(2)])
    vs = Ring([AR.alloc([1024], BF16, "vd%d" % i) for i in range(4)])
    t1s = Ring([AR.alloc([1024], F32, "t1_%d" % i) for i in range(2)])
    t2s = Ring([AR.alloc([1024], F32, "t2_%d" % i) for i in range(3)])
    yns = Ring([AR.alloc([1024], BF16, "yn%d" % i) for i in range(2)])
    ysts = Ring([AR.alloc([8, 128], BF16, "yst%d" % i) for i in range(2)])
    MRs = Ring([AR.alloc([8, 128], BF16, "MR%d" % i) for i in range(1)])
    qfs = Ring([AR.alloc([4, 128], BF16, "qfb%d" % i) for i in range(6)])
    kws = Ring([AR.alloc([512], BF16, "kwb%d" % i) for i in range(1)])
    s8 = Ring([AR.alloc([4, 8], F32, "s8_%d" % i) for i in range(2)])
    tSs = Ring([AR.alloc([1024], F32, "uS%d" % i) for i in range(1)])
    tRs = Ring([AR.alloc([512], F32, "uR%d" % i) for i in range(1)])
    ssq = Ring([AR.alloc([4], F32, "ssq%d" % i) for i in range(2)])
    order = [1, 0] + list(range(NCH - 1, 1, -1))
    fmv = lambda T_: T_.rearrange("(g p) t -> p g t", p=128)
    for c in order:
        rows = slice(c * 128, (c + 1) * 128)
        ld = lds.next()
        for nm, src in (("dtl", n.DTL), ("x", n.X_tm), ("b", n.B_tm), ("z", n.Zs_tm), ("k", n.K_tm), ("v", n.V_tm), ("g", n.G_tm)):
            P.dma(SP, ld[nm].ap, src[rows, :], writes=[ld[nm].key], semkey=ld[nm].key)
        for nm, src in (("bf", n.B_fm), ("cf", n.C_fm), ("kf", n.K_fm), ("qf", n.Q_fm)):
            P.dma(SP, ld[nm].ap, fmv(src)[:, :, rows], writes=[ld[nm].key], semkey=ld[nm].key)
        P.dma(SP, ld["hs"].ap, n.HFs[c], writes=[ld["hs"].key], semkey=ld["hs"].key)
        P.dma(SP, ld["hr"].ap, n.HFr[c], writes=[ld["hr"].key], semkey=ld["hr"].key)
        dtl = ld["dtl"]
        psS = nxt()

        def f(e, psS=psS, dtl=dtl):
            e.matmul(psS.ap[:, 0:32], one_f, dtl.ap[:, 32:64], start=True, stop=True)
            e.matmul(psS.ap[:, 32:48], C("tu"), dtl.ap[:, 32:48], start=True, stop=True)
            return e.matmul(psS.ap[:, 48:64], C("tl"), dtl.ap[:, 48:64], start=True, stop=True)
        P.add(PE, f, ["cst", dtl.key], [psS.key])
        sm = sms.next()
        smf = sm.ap.rearrange("p a b -> p (a b)")
        act(smf[:, 0:32], psS.ap[:, 32:64], AF.Exp, [psS.key], [(sm.key, 0)])
        act(sm.ap[:, 2, :], psS.ap[:, 16:32], AF.Identity, [psS.key], [(sm.key, 2)])
        tt(DVE, sm.ap[:, 3, :], sm.ap[:, 2, :], psS.ap[:, 48:64], ALU.subtract, [(sm.key, 2), psS.key], [(sm.key, 3)])
        act(sm.ap[:, 3, :], sm.ap[:, 3, :], AF.Exp, [(sm.key, 3)], [(sm.key, 3)])
        act(sm.ap[:, 4, :], sm.ap[:, 2, :], AF.Exp, [(sm.key, 2)], [(sm.key, 4)])
        tt(DVE, sm.ap[:, 5, :], dtl.ap[:, 16:32], sm.ap[:, 3, :], ALU.mult, [dtl.key, (sm.key, 3)], [(sm.key, 5)])
        if c == order[0]:
            for q_ in range(6):
                n.dump(sm.ap[:, q_, :], (sm.key, 0 if q_ < 2 else q_), q_ * 16, 16)
        if n.E2L <= 1:
            continue
        pss = nxt()

        def f(e, pss=pss, ld=ld):
            for gq in range(4):
                ins = e.matmul(pss.ap[:, gq * 128:(gq + 1) * 128], ld["bf"].ap[:, gq, :], ld["cf"].ap[:, gq, :], start=True, stop=True)
            return ins
        P.add(PE, f, [ld["bf"].key, ld["cf"].key], [pss.key])
        Md, vd, scm_d = [], [], []
        for d in range(2):
            scm = scms.next()
            tt(DVE, scm.ap, v3(pss.ap, 4), bc(C("tu" if d == 0 else "tl").unsqueeze(1), [128, 4, 128]), ALU.mult, [pss.key, "cst"], [scm.key])
            scm_d.append(scm)
        for d in range(2):
            tri, stri = ("tu", "su") if d == 0 else ("tl", "sl")
            Rd, Ed = Rs.next(), Es.next()
            tt(POOL, Rd.ap, bc(C(tri).unsqueeze(1), [128, 16, 128]), bc(dtl.ap[:, 32 + 16 * d:48 + 16 * d].unsqueeze(2), [128, 16, 128]),
               ALU.mult, ["cst", dtl.key], [Rd.key])
            Rf = Rd.ap.rearrange("p a b -> p (a b)")
            Ef = Ed.ap.rearrange("p a b -> p (a b)")
            for q in range(4):
                psq = nxt()

                def f(e, psq=psq, Rf=Rf, q=q, stri=stri):
                    return e.matmul(psq.ap, C(stri), Rf[:, q * 512:(q + 1) * 512], start=True, stop=True)
                P.add(PE, f, ["cst", Rd.key], [psq.key])
                act(Ef[:, q * 512:(q + 1) * 512], psq.ap, AF.Exp, [psq.key], [Ed.key])
            scm = scm_d[d]
            M = Ms.next()
            tt(DVE, M.ap.rearrange("p (g h) i -> p g h i", g=4), Ed.ap.rearrange("p (g h) i -> p g h i", g=4),
               bc(scm.ap.unsqueeze(2), [128, 4, 4, 128]), ALU.mult, [Ed.key, scm.key], [M.key])
            v_ = vs.next()
            tt(DVE, v3(v_.ap, 16), v3(ld["x"].ap, 16), bc(dtl.ap[:, 16 * d:16 * d + 16].unsqueeze(2), [128, 16, 64]), ALU.mult,
               [ld["x"].key, dtl.key], [v_.key])
            if c == order[0] and d == 0:
                n.dump(Rf, Rd.key, 128, 2048)
                n.dump(Ef, Ed.key, 128 + 2048, 2048)
                n.dump(scm.ap.rearrange("p a b -> p (a b)"), scm.key, 128 + 4096, 512)
                n.dump(M.ap.rearrange("p a b -> p (a b)"), M.key, 0, 2048, True)
                n.dump(v_.ap, v_.key, 2048, 1024, True)
            Md.append(M)
            vd.append(v_)
        if n.E2L <= 2:
            continue
        pY = [nxt(), nxt()]

        def f(e, pY=pY, Md=Md, vd=vd):
            for h in range(16):
                for d in range(2):
                    ins = e.matmul(pY[h // 8].ap[:, (h % 8) * 64:(h % 8) * 64 + 64], Md[d].ap[:, h, :], vd[d].ap[:, h * 64:(h + 1) * 64],
                                   start=(d == 0), stop=(d == 1))
            return ins
        P.add(PE, f, [Md[0].key, Md[1].key, vd[0].key, vd[1].key], [pY[0].key, pY[1].key])
        pF = [nxt(), nxt()]

        def f(e, pF=pF, ld=ld):
            for gq in range(4):
                ins = e.matmul(pF[gq // 2].ap[:, (gq % 2) * 256:(gq % 2) * 256 + 256], ld["cf"].ap[:, gq, :],
                               ld["hs"].ap[:, gq * 256:(gq + 1) * 256], start=True, stop=True)
            return ins
        P.add(PE, f, [ld["cf"].key, ld["hs"].key], [pF[0].key, pF[1].key])
        t1, t2 = t1s.next(), t2s.next()
        for bk in range(2):
            tt(DVE, v3(t1.ap[:, bk * 512:(bk + 1) * 512], 8), v3(pF[bk].ap, 8), bc(sm.ap[:, 0, bk * 8:bk * 8 + 8].unsqueeze(2), [128, 8, 64]),
               ALU.mult, [pF[bk].key, (sm.key, 0)], [t1.key])
        pB = [nxt(), nxt()]

        def f(e, pB=pB, ld=ld):
            for gq in range(4):
                ins = e.matmul(pB[gq // 2].ap[:, (gq % 2) * 256:(gq % 2) * 256 + 256], ld["cf"].ap[:, gq, :],
                               gSb.ap[:, gq * 256:(gq + 1) * 256], start=True, stop=True)
            return ins
        P.add(PE, f, [ld["cf"].key, "gSb"], [pB[0].key, pB[1].key])
        for bk in range(2):
            tt(DVE, v3(t2.ap[:, bk * 512:(bk + 1) * 512], 8), v3(pB[bk].ap, 8), bc(sm.ap[:, 1, bk * 8:bk * 8 + 8].unsqueeze(2), [128, 8, 64]),
               ALU.mult, [pB[bk].key, (sm.key, 0)], [t2.key])
        if c == order[0]:
            n.dump(t1.ap, t1.key, 5000, 1024)
            n.dump(t2.ap, t2.key, 6024, 1024)
        tt(POOL, t1.ap, t1.ap, t2.ap, ALU.add, [t1.key, t2.key], [t1.key])
        for bk in range(2):
            tt(DVE, t1.ap[:, bk * 512:(bk + 1) * 512], t1.ap[:, bk * 512:(bk + 1) * 512], pY[bk].ap, ALU.add, [t1.key, pY[bk].key], [t1.key])
        if c == order[0]:
            n.dump(t1.ap, t1.key, 7048, 1024)
        t3 = t2s.next()
        tt(POOL, v3(t3.ap, 16), v3(ld["x"].ap, 16), bc(rp.ap[:, R_SD:R_SD + 16].unsqueeze(2), [128, 16, 64]), ALU.mult, [ld["x"].key, rp.key], [t3.key])
        tt(POOL, t1.ap, t1.ap, t3.ap, ALU.add, [t1.key, t3.key], [t1.key])
        tt(DVE, t1.ap, t1.ap, ld["z"].ap, ALU.mult, [t1.key, ld["z"].key], [t1.key])
        t4 = t2s.next()
        sq_ = ssq.next()
        act(t4.ap, t1.ap, AF.Square, [t1.key], [t4.key])
        P.add(DVE, lambda e, sq_=sq_, t4=t4: e.reduce_sum(sq_.ap[:, 0:1], t4.ap, AX.X), [t4.key], [(sq_.key, 0)])
        act(sq_.ap[:, 1:2], sq_.ap[:, 0:1], AF.Ln, [(sq_.key, 0)], [(sq_.key, 1)], bias=n.eps_ap, scale=1.0 / 1024)
        act(sq_.ap[:, 2:3], sq_.ap[:, 1:2], AF.Exp, [(sq_.key, 1)], [(sq_.key, 2)], scale=-0.5)
        yn = yns.next()
        stt(DVE, yn.ap, t1.ap, sq_.ap[:, 2:3], rp.ap[:, R_NW:R_NW + 1024], ALU.mult, ALU.mult, [t1.key, (sq_.key, 2), rp.key], [yn.key])
        if c == order[0]:
            n.dump(t1.ap, t1.key, 8072, 1024)
            n.dump(yn.ap, yn.key, 3072, 1024, True)
        yst = ysts.next()
        transposes_to(n, yn.ap, 8, yn.key, yst.ap, yst.key)
        P.dma(SP, fmv(n.YS_fm)[:, :, rows], yst.ap, reads=[yst.key], semkey=("st", yst.key))
        if n.E2L <= 3:
            continue
        vwb = vs.next()
        tt(DVE, v3(vwb.ap, 16), v3(ld["x"].ap, 16), bc(sm.ap[:, 5, :].unsqueeze(2), [128, 16, 64]), ALU.mult, [ld["x"].key, (sm.key, 5)], [vwb.key])
        pa, pb = nxt(), nxt()

        def f(e, pa=pa, pb=pb, ld=ld, vwb=vwb):
            for gq in range(4):
                ins = e.matmul((pa if gq < 2 else pb).ap[:, (gq % 2) * 256:(gq % 2) * 256 + 256], ld["b"].ap[:, gq * 128:(gq + 1) * 128],
                               vwb.ap[:, gq * 256:(gq + 1) * 256], start=True, stop=True)
            return ins
        P.add(PE, f, [ld["b"].key, vwb.key], [pa.key, pb.key])
        tS = tSs.next()
        tt(POOL, v3(tS.ap, 16), v3(gS.ap, 16), bc(sm.ap[:, 4, :].unsqueeze(2), [128, 16, 64]), ALU.mult, ["gS", (sm.key, 4)], [tS.key])
        tt(DVE, gS.ap[:, 0:512], tS.ap[:, 0:512], pa.ap, ALU.add, [tS.key, pa.key], ["gS"])
        tt(DVE, gS.ap[:, 512:1024], tS.ap[:, 512:1024], pb.ap, ALU.add, [tS.key, pb.key, "gS"], ["gS"])
        act(gSb.ap, gS.ap, AF.Identity, ["gS"], ["gSb"])
        if n.E2L <= 4:
            continue
        pR = [nxt(), nxt()]
        qm = [qfs.next(), qfs.next()]
        for pr_ in range(2):
            ts(DVE, qm[pr_].ap, ld["qf"].ap, n.cst_sb.ap[:, K_HM + pr_:K_HM + pr_ + 1], None, ALU.mult, None, [ld["qf"].key, "cst"], [qm[pr_].key])

        def f(e, pR=pR, ld=ld, qm=qm):
            for h in range(8):
                ins = e.matmul(pR[h // 4].ap[:, (h % 4) * 128:(h % 4) * 128 + 128], ld["kf"].ap[:, h // 2, :],
                               qm[h % 2].ap[:, h // 2, :], start=True, stop=True)
            return ins
        P.add(PE, f, [ld["kf"].key, qm[0].key, qm[1].key], [pR[0].key, pR[1].key])
        MR = MRs.next()
        for bk in range(2):
            tt(DVE, MR.ap[:, bk * 4:bk * 4 + 4, :], v3(pR[bk].ap, 4), n.DtabR.ap[:, bk * 4:bk * 4 + 4, :], ALU.mult, [pR[bk].key, "DtabR"], [MR.key])
        qfm_ = [qfs.next(), qfs.next()]
        qbm_ = [qfs.next(), qfs.next()]
        for pr_ in range(2):
            tt(DVE, qfm_[pr_].ap, qm[pr_].ap, n.Gfb.ap[:, 0], ALU.mult, [qm[pr_].key, "Gfb"], [qfm_[pr_].key])
            tt(DVE, qbm_[pr_].ap, qm[pr_].ap, n.Gfb.ap[:, 1], ALU.mult, [qm[pr_].key, "Gfb"], [qbm_[pr_].key])
        if n.E2L <= 5:
            continue
        pZ = [nxt(), nxt()]

        def f(e, pZ=pZ, ld=ld, MR=MR, qfm_=qfm_, qbm_=qbm_):
            for h in range(8):
                pr = h // 2
                o = pZ[h // 4].ap[:, (h % 4) * 128:(h % 4) * 128 + 128]
                e.matmul(o, MR.ap[:, h, :], ld["v"].ap[:, h * 128:(h + 1) * 128], start=True, stop=False)
                e.matmul(o, qfm_[h % 2].ap[:, pr, :], ld["hr"].ap[:, pr * 128:(pr + 1) * 128], start=False, stop=False)
                ins = e.matmul(o, qbm_[h % 2].ap[:, pr, :], gRb.ap[:, pr * 128:(pr + 1) * 128], start=False, stop=True)
            return ins
        P.add(PE, f, [MR.key, ld["v"].key, qfm_[0].key, qfm_[1].key, qbm_[0].key, qbm_[1].key, ld["hr"].key, "gRb"], [pZ[0].key, pZ[1].key])
        if n.E2L <= 6:
            continue
        yr, cen = t2s.next(), t2s.next()
        for bk in range(2):
            act(yr.ap[:, bk * 512:(bk + 1) * 512], pZ[bk].ap, AF.Identity, [pZ[bk].key], [yr.key])
        st8 = s8.next()
        P.add(DVE, lambda e, st8=st8, yr=yr: e.reduce_sum(st8.ap[:, 0, :], v3(yr.ap, 8), AX.X), [yr.key], [(st8.key, 0)])
        ts(DVE, st8.ap[:, 1, :], st8.ap[:, 0, :], -1.0 / 128, None, ALU.mult, None, [(st8.key, 0)], [(st8.key, 1)])
        tt(DVE, v3(cen.ap, 8), v3(yr.ap, 8), bc(st8.ap[:, 1, :].unsqueeze(2), [128, 8, 128]), ALU.add, [yr.key, (st8.key, 1)], [cen.key])
        sqr = t2s.next()
        act(sqr.ap, cen.ap, AF.Square, [cen.key], [sqr.key])
        P.add(DVE, lambda e, st8=st8, sqr=sqr: e.reduce_sum(st8.ap[:, 2, :], v3(sqr.ap, 8), AX.X), [sqr.key], [(st8.key, 2)])
        act(st8.ap[:, 3, :], st8.ap[:, 2, :], AF.Ln, [(st8.key, 2)], [(st8.key, 3)], bias=n.eps_ap, scale=1.0 / 128)
        act(st8.ap[:, 3, :], st8.ap[:, 3, :], AF.Exp, [(st8.key, 3)], [(st8.key, 3)], scale=-0.5)
        tt(DVE, v3(cen.ap, 8), v3(cen.ap, 8), bc(st8.ap[:, 3, :].unsqueeze(2), [128, 8, 128]), ALU.mult, [cen.key, (st8.key, 3)], [cen.key])
        yo = yns.next()
        tt(POOL, yo.ap, cen.ap, ld["g"].ap, ALU.mult, [cen.key, ld["g"].key], [yo.key])
        yst = ysts.next()
        transposes_to(n, yo.ap, 8, yo.key, yst.ap, yst.key)
        P.dma(SP, fmv(n.YR_fm)[:, :, rows], yst.ap, reads=[yst.key], semkey=("st", yst.key))
        if n.E2L <= 7:
            continue
        kw = kws.next()
        tt(DVE, v3(kw.ap, 8), v3(ld["k"].ap, 8), bc(n.Wfb.ap[:, 1, :].unsqueeze(2), [128, 8, 64]), ALU.mult, [ld["k"].key, "Wfb"], [kw.key])
        ret_state_update(n, kw, ld["v"], gR, "gR", 1, tRs)
        cpy(POOL, gRb.ap, gR.ap, ["gR"], ["gRb"])
    P.barrier()
    AR.off = base


def ret_state_update(n, kw, v, hR, hkey, d, tRs):
    P, nxt = n.P, n.psA_ring.next
    pc, pd = nxt(), nxt()

    def f(e, pc=pc, pd=pd):
        for pr in range(4):
            ins = e.matmul((pc if pr < 2 else pd).ap[:, (pr % 2) * 256:(pr % 2) * 256 + 256], kw.ap[:, pr * 128:(pr + 1) * 128],
                           v.ap[:, pr * 256:(pr + 1) * 256], start=True, stop=True)
        return ins
    P.add(PE, f, [kw.key, v.key], [pc.key, pd.key])
    tR = tRs.next()
    h4 = hR.ap.rearrange("p (q c) -> p q c", q=4)
    t4 = tR.ap.rearrange("p (q c) -> p q c", q=4)
    n.tt(POOL, t4, h4, n.bc(n.CDsel.ap[:, d, :].unsqueeze(2), [128, 4, 128]), ALU.mult, [hkey, "CDsel"], [tR.key])
    for bk, ps_ in ((0, pc), (1, pd)):
        psv = ps_.ap.rearrange("p (q c) -> p q c", q=2)
        n.tt(DVE, h4[0:64, 2 * bk:2 * bk + 2, :], t4[0:64, 2 * bk:2 * bk + 2, :], psv[0:64, :, 0:128], ALU.add, [tR.key, ps_.key, hkey], [hkey])
        n.tt(DVE, h4[64:128, 2 * bk:2 * bk + 2, :], t4[64:128, 2 * bk:2 * bk + 2, :], psv[64:128, :, 128:256], ALU.add, [tR.key, ps_.key, hkey], [hkey])


def build_merge(g):
    n = NS(g)
    P, AR, l, cp, mv = n.P, n.AR, n.l, n.cp, n.mv
    nxt = n.psA_ring.next
    h1d_v = n.h1d.rearrange("(c p) t -> p c t", p=128)
    ysrc = [T_.rearrange("(c p) t -> p c t", p=128) for T_ in (n.YS_fm, n.YP_fm, n.YC_fm, n.YR_fm)]
    for si, (s0, s1) in enumerate(n.streams):
        if si == 0 and n.last:
            continue
        for (t0, Tb) in n.blocks_of(s0, s1, 512):
            base = AR.off
            h1 = AR.alloc([DC, Tb], BF16, "h1b")
            yb = [AR.alloc([8, Tb], BF16, "yb%d" % i) for i in range(4)]
            mg = AR.alloc([DC, Tb], BF16, "mg")
            gts = Ring([AR.alloc([DC, 256], BF16, "gt%d" % i) for i in range(3)])
            bts = Ring([AR.alloc([8, 256], BF16, "bt%d" % i) for i in range(3)])
            wos = Ring([AR.alloc([DC, 256], BF16, "wo%d" % i) for i in range(2)])
            sigs = Ring([AR.alloc([512], F32, "sig%d" % i) for i in range(2)])
            accs = Ring([AR.alloc([512], F32, "acc%d" % i) for i in range(4)])
            tms_ = Ring([AR.alloc([512], F32, "mt%d" % i) for i in range(2)])
            xts = Ring([AR.alloc([512], F32, "mxt%d" % i) for i in range(2)])
            xos = Ring([AR.alloc([512], F32, "mxo%d" % i) for i in range(2)])
            P.dma(SP, h1.ap, h1d_v[:, :, t0:t0 + Tb], writes=[h1.key], semkey=h1.key)
            for i in range(4):
                P.dma(SP, yb[i].ap, ysrc[i][:, :, t0:t0 + Tb], writes=[yb[i].key], semkey=yb[i].key)
            for mp in range(8):
                acc = [accs.next(), accs.next()]
                for i in range(4):
                    gt, bt = gts.next(), bts.next()
                    P.dma(POOL, gt.ap, n.w_gate[l, i, :, mp * 256:(mp + 1) * 256].rearrange("(k p) m -> p k m", p=128), writes=[gt.key], semkey=gt.key)
                    P.dma(POOL, bt.ap, n.w_branch[l, i, :, mp * 256:(mp + 1) * 256].rearrange("(k p) m -> p k m", p=128), writes=[bt.key], semkey=bt.key)
                    for mm in range(2):
                        m = mp * 2 + mm
                        psg, psb = nxt(), nxt()

                        def f(e, psg=psg, gt=gt, mm=mm, h1=h1, Tb=Tb):
                            for k in range(DC):
                                ins = e.matmul(psg.ap[:, 0:Tb], gt.ap[:, k, mm * 128:(mm + 1) * 128], h1.ap[:, k, :], start=(k == 0), stop=(k == DC - 1))
                            return ins
                        P.add(PE, f, [gt.key, h1.key], [psg.key])

                        def f(e, psb=psb, bt=bt, mm=mm, i=i, yb=yb, Tb=Tb):
                            for k in range(8):
                                ins = e.matmul(psb.ap[:, 0:Tb], bt.ap[:, k, mm * 128:(mm + 1) * 128], yb[i].ap[:, k, :], start=(k == 0), stop=(k == 7))
                            return ins
                        P.add(PE, f, [bt.key, yb[i].key], [psb.key])
                        sg = sigs.next()
                        col = C_BG + i * 16 + m
                        n.act(sg.ap[:, 0:Tb], psg.ap[:, 0:Tb], AF.Sigmoid, [psg.key, cp.key], [sg.key], bias=cp.ap[:, col:col + 1])
                        if i == 0:
                            n.tt(DVE, acc[mm].ap[:, 0:Tb], sg.ap[:, 0:Tb], psb.ap[:, 0:Tb], ALU.mult, [sg.key, psb.key], [acc[mm].key])
                        else:
                            tm = tms_.next()
                            n.tt(DVE, tm.ap[:, 0:Tb], sg.ap[:, 0:Tb], psb.ap[:, 0:Tb], ALU.mult, [sg.key, psb.key], [tm.key])
                            n.tt(POOL, acc[mm].ap[:, 0:Tb], acc[mm].ap[:, 0:Tb], tm.ap[:, 0:Tb], ALU.add, [acc[mm].key, tm.key], [acc[mm].key])
                for mm in range(2):
                    n.act(mg.ap[:, mp * 2 + mm, :], acc[mm].ap[:, 0:Tb], AF.Identity, [acc[mm].key], [("mg", mp * 2 + mm)])
            mgk = [("mg", m) for m in range(DC)]
            for mp in range(8):
                wo = wos.next()
                P.dma(POOL, wo.ap, n.w_o[l, :, mp * 256:(mp + 1) * 256].rearrange("(k p) m -> p k m", p=128), writes=[wo.key], semkey=wo.key)
                for mm in range(2):
                    m = mp * 2 + mm
                    xt, xo = xts.next(), xos.next()
                    P.dma(SP, xt.ap[:, 0:Tb], n.xres[m * 128:(m + 1) * 128, t0:t0 + Tb], writes=[xt.key], semkey=xt.key)
                    ps = nxt()

                    def f(e, ps=ps, wo=wo, mm=mm, mg=mg, Tb=Tb):
                        for k in range(DC):
                            ins = e.matmul(ps.ap[:, 0:Tb], wo.ap[:, k, mm * 128:(mm + 1) * 128], mg.ap[:, k, :], start=(k == 0), stop=(k == DC - 1))
                        return ins
                    P.add(PE, f, [wo.key] + mgk, [ps.key])
                    n.stt(DVE, xo.ap[:, 0:Tb], ps.ap[:, 0:Tb], mv.ap[:, 32 + m, si:si + 1], xt.ap[:, 0:Tb], ALU.mult, ALU.add,
                          [ps.key, mv.key, xt.key], [xo.key])
                    P.dma(SP, n.xmid[m * 128:(m + 1) * 128, t0:t0 + Tb], xo.ap[:, 0:Tb], reads=[xo.key], semkey=("st", xo.key))
            P.barrier()
            AR.off = base


def build_ffn(g):
    n = NS(g)
    P, AR, l, cp, mv = n.P, n.AR, n.l, n.cp, n.mv
    nxt = n.psA_ring.next
    Wup = n.ffn_up[l]
    NF = DFF // 128
    for si, (s0, s1) in enumerate(n.streams):
        if si == 0 and n.last:
            continue
        for (t0, Tb) in n.blocks_of(s0, s1, 512):
            W = Tb + 16
            haloL, haloR = t0 > s0, t0 + Tb < s1
            base0 = AR.off
            hext = AR.alloc([DC, W], BF16, "hext")
            base1 = AR.off
            norm_block(n, hext, n.xmid_v, t0, Tb, s0, s1, n.A2[n.par], mv.ap[:, 48:64, :], mv.key, si)
            P.barrier()
            AR.off = base1
            actb = AR.alloc([NF, Tb], BF16, "actb")
            wus = Ring([AR.alloc([DC, 256], BF16, "wu%d" % i) for i in range(4)])
            wds = Ring([AR.alloc([NF, 128], BF16, "wd%d" % i) for i in range(2)])
            raws = Ring([AR.alloc([W], F32, "fr%d" % i) for i in range(4)])
            tmps = Ring([AR.alloc([W], F32, "ft%d" % i) for i in range(6)])
            xts = Ring([AR.alloc([512], F32, "fxt%d" % i) for i in range(2)])
            xos = Ring([AR.alloc([512], F32, "fxo%d" % i) for i in range(2)])
            for jp in range(NF // 2):
                wa, wb = wus.next(), wus.next()
                P.dma(POOL, wa.ap, Wup[:, jp * 256:(jp + 1) * 256].rearrange("(k p) m -> p k m", p=128), writes=[wa.key], semkey=wa.key)
                P.dma(POOL, wb.ap, Wup[:, DFF + jp * 256:DFF + (jp + 1) * 256].rearrange("(k p) m -> p k m", p=128), writes=[wb.key], semkey=wb.key)
                for cc in range(2):
                    j = jp * 2 + cc
                    rs_ = []
                    for (ws, ch) in ((wa, j), (wb, NF + j)):
                        raw = raws.next()
                        fm_gemm(n, hext, Tb, haloL, haloR, ws, cc * 128, raw)
                        r = conv3(n, raw, raw_keys(raw, Tb), Tb, cp.ap[:, C_FW + ch:C_FW + ch + 1], cp.ap[:, C_FW + 88 + ch:C_FW + 89 + ch],
                                  cp.ap[:, C_FW + 176 + ch:C_FW + 177 + ch], cp.ap[:, C_FB + ch:C_FB + ch + 1], tmps, cp.key)
                        rs_.append(r)
                    sa = tmps.next()
                    n.act(sa.ap[:, 0:Tb], rs_[0].ap[:, 0:Tb], AF.Silu, [rs_[0].key], [sa.key])
                    n.tt(DVE, actb.ap[:, j, :], sa.ap[:, 0:Tb], rs_[1].ap[:, 0:Tb], ALU.mult, [sa.key, rs_[1].key], [("actb", j)])
            ak = [("actb", j) for j in range(NF)]
            for m in range(DC):
                wd = wds.next()
                P.dma(POOL, wd.ap, n.ffn_down[l, :, m * 128:(m + 1) * 128].rearrange("(k p) m -> p k m", p=128), writes=[wd.key], semkey=wd.key)
                xt, xo = xts.next(), xos.next()
                P.dma(SP, xt.ap[:, 0:Tb], n.xmid[m * 128:(m + 1) * 128, t0:t0 + Tb], writes=[xt.key], semkey=xt.key)
                ps = nxt()

                def f(e, ps=ps, wd=wd, actb=actb, Tb=Tb):
                    for k in range(NF):
                        ins = e.matmul(ps.ap[:, 0:Tb], wd.ap[:, k, :], actb.ap[:, k, :], start=(k == 0), stop=(k == NF - 1))
                    return ins
                P.add(PE, f, [wd.key] + ak, [ps.key])
                n.stt(DVE, xo.ap[:, 0:Tb], ps.ap[:, 0:Tb], mv.ap[:, 80 + m, si:si + 1], xt.ap[:, 0:Tb], ALU.mult, ALU.add,
                      [ps.key, mv.key, xt.key], [xo.key])
                P.dma(SP, n.xres[m * 128:(m + 1) * 128, t0:t0 + Tb], xo.ap[:, 0:Tb], reads=[xo.key], semkey=("st", xo.key))
            P.barrier()
            AR.off = base0


def build_final(g):
    n = NS(g)
    P, AR = n.P, n.AR
    fin_sb = AR.alloc([DC], F32, "fin")
    P.dma(SP, fin_sb.ap, n.fin[:, :], writes=["fin"], semkey="fin")
    xt_ring = Ring([AR.alloc([DC, 256], F32, "ox%d" % i) for i in range(2)])
    sq_ring = Ring([AR.alloc([DC, 256], BF16, "os%d" % i) for i in range(2)])
    rs_ring = Ring([AR.alloc([256], F32, "or%d" % i) for i in range(2)])
    oo_ring = Ring([AR.alloc([DC, 256], F32, "oo%d" % i) for i in range(2)])
    outv = n.outT.rearrange("(c p) t -> p c t", p=128)
    for ta in range(TC, n.T, 256):
        w = 256
        xt, sq, rs, oo = xt_ring.next(), sq_ring.next(), rs_ring.next(), oo_ring.next()
        P.dma(SP, xt.ap, n.xres_v[:, :, ta:ta + w], writes=[xt.key], semkey=xt.key)
        n.act(sq.ap, xt.ap, AF.Square, [xt.key], [sq.key])
        ps = n.psA_ring.next()

        def f(e, ps=ps, sq=sq):
            for k in range(DC):
                ins = e.matmul(ps.ap[:, 0:w], n.ones_bf.ap, sq.ap[:, k, :], start=(k == 0), stop=(k == DC - 1))
            return ins
        P.add(PE, f, [sq.key, "ones_bf"], [ps.key])
        n.act(rs.ap, ps.ap[:, 0:w], AF.Ln, [ps.key], [rs.key], bias=n.eps_ap, scale=1.0 / D)
        n.act(rs.ap, rs.ap, AF.Exp, [rs.key], [rs.key], scale=-0.5)
        for k in range(DC):
            n.stt(DVE, oo.ap[:, k, :], xt.ap[:, k, :], fin_sb.ap[:, k:k + 1], rs.ap, ALU.mult, ALU.mult, [xt.key, rs.key, "fin"], [oo.key])
        P.dma(SP, outv[:, :, ta - TC:ta - TC + w], oo.ap, reads=[oo.key], semkey=("st", oo.key))


def _consts():
    c = np.zeros((128, NCST), np.float32)
    j = np.arange(128)[:, None].astype(np.float32)
    i = np.arange(128)[None, :].astype(np.float32)
    c[:, K_ID:K_ID + 128] = (j == i)
    c[:, K_ONE:K_ONE + 128] = 1.0
    c[:, K_TU:K_TU + 128] = (j <= i)
    c[:, K_TL:K_TL + 128] = (j >= i)
    c[:, K_SU:K_SU + 128] = (j > i)
    c[:, K_SL:K_SL + 128] = (j < i)
    c[:, K_DP:K_DP + 128] = np.maximum(i - j, 0)
    c[:, K_DN:K_DN + 128] = np.maximum(j - i, 0)
    c[:, K_I1:K_I1 + 128] = i + 1
    c[:, K_IR:K_IR + 128] = 128 - i
    c[:, K_PJ127] = 127 - np.arange(128)
    c[:, K_PJ] = np.arange(128)
    c[0:64, K_HM] = 1.0
    c[64:128, K_HM + 1] = 1.0
    for gq, win in enumerate(POOL_WINDOWS):
        half = win // 2
        for t in range(half):
            c[:, K_PINV + (gq * 2) * 8 + t] = 1.0 / (t + half)
        for jj in range(half - 1):
            c[:, K_PINV + (gq * 2 + 1) * 8 + jj] = 1.0 / (win - 1 - jj)
    return c


def _rope(TL):
    t = np.arange(TL)
    row = (t // 64).astype(np.float32)
    col = (t % 64).astype(np.float32)
    inv = (np.float32(10000.0) ** (-np.arange(16, dtype=np.float32) / np.float32(16))).astype(np.float32)
    ar = (row[:, None] * inv[None, :]).astype(np.float32)
    ac = (col[:, None] * inv[None, :]).astype(np.float32)
    cr, sr, cc, sc = np.cos(ar), np.sin(ar), np.cos(ac), np.sin(ac)
    return np.concatenate([cr, cr, cc, cc, -sr, sr, -sc, sc], axis=1).astype(np.float32)


def _col(v):
    v = np.asarray(v, np.float32)
    return v.reshape(-1, 128).T


def _packs(inp, L):
    colp = np.zeros((L, 128, NCOL), np.float32)
    rowp = np.zeros((L, 128, NROW), np.float32)
    for l in range(L):
        colp[l, :, C_N1:C_N1 + 16] = _col(inp["norm1_w"][l])
        colp[l, :, C_N2:C_N2 + 16] = _col(inp["norm2_w"][l])
        colp[l, :, C_BM:C_BM + 96] = _col(inp["b_mod"][l])
        for j in range(3):
            colp[l, :, C_CW + 16 * j:C_CW + 16 * j + 16] = _col(inp["ssd_conv_w"][l, j])
            colp[l, :, C_SW + 8 * j:C_SW + 8 * j + 8] = _col(inp["sconv_w"][l, j])
            colp[l, :, C_FW + 88 * j:C_FW + 88 * j + 88] = _col(inp["ffn_conv_w"][l, j])
        colp[l, :, C_CB:C_CB + 16] = _col(inp["ssd_conv_b"][l])
        colp[l, :, C_PS:C_PS + 8] = _col(inp["pool_scale"][l])
        for i in range(4):
            colp[l, :, C_BG + 16 * i:C_BG + 16 * i + 16] = _col(inp["b_gate"][l, i])
        colp[l, :, C_FB:C_FB + 88] = _col(inp["ffn_conv_b"][l])
        rowp[l, :, R_DB:R_DB + 32] = np.asarray(inp["ssd_dt_bias"][l]).reshape(32)[None, :]
        rowp[l, :, R_AL:R_AL + 32] = np.asarray(inp["ssd_a_log"][l]).reshape(32)[None, :]
        rowp[l, :, R_SD:R_SD + 16] = np.asarray(inp["ssd_d"][l]).reshape(16)[None, :]
        rowp[l, :, R_RL:R_RL + 16] = np.asarray(inp["ret_decay_logit"][l]).reshape(16)[None, :]
        rowp[l, :, R_NW:R_NW + 1024] = np.asarray(inp["ssd_norm_w"][l]).reshape(1024)[None, :]
    return colp, rowp


def make_in_maps(inp, NLC, L, n_cores=8):
    TL = NLC * 128
    f = lambda a: np.ascontiguousarray(np.asarray(a, np.float32))
    colp, rowp = _packs(inp, L)
    shared = dict(w_mod=f(inp["w_mod"][:L]), w_in=f(inp["w_in"][:L]), pool_w=f(inp["pool_w"][:L]), w_branch=f(inp["w_branch"][:L]),
                  w_gate=f(inp["w_gate"][:L]), w_o=f(inp["w_o"][:L]), ffn_up=f(inp["ffn_up"][:L]), ffn_down=f(inp["ffn_down"][:L]),
                  colp=colp, rowp=rowp, fin=f(_col(inp["final_norm_w"])), cst=_consts(), rope=_rope(TL))
    B = inp["x"].shape[0]
    maps = []
    for i in range(n_cores):
        b = i % B
        cvec = np.stack([_col(inp["c_ctx"]), _col(inp["c"][b])], axis=2)
        m = dict(shared)
        m["xT"] = f(np.asarray(inp["x"][b, :TL]).T)
        m["cT"] = f(np.asarray(inp["ctx"][b]).T)
        m["cvec"] = f(cvec)
        maps.append(m)
    return maps


_CACHE = {}


def kernel(**inputs):
    NLC, L = 32, 4
    if "nc" not in _CACHE:
        _CACHE["nc"] = build(NLC, L)[0]
    nc = _CACHE["nc"]
    maps = make_in_maps(inputs, NLC, L)
    res = run_bass_kernel_spmd(nc, maps, core_ids=list(range(8)))
    B = inputs["x"].shape[0]
    out = np.stack([np.ascontiguousarray(res.results[b]["outT"].T) for b in range(B)], axis=0)
    return out.astype(np.float32)
```
